# Optimizing a Trainium2 kernel written in Bass

```python
import jax, jax.numpy as jnp
from jax import lax
import numpy as np

D_MODEL = 1024
BATCH = 4
SEQ = 4096
DEPTH = 2

GRID_W = 64
MEM_LEN = 256
HG_HEADS = 8
HG_DIM = 128
HG_WIDTH = HG_HEADS * HG_DIM
HG_CHUNK = 64
NA_HEADS = 8
NA_DIM = 64
NA_WIDTH = NA_HEADS * NA_DIM
NA_KH = 8
NA_KW = 16
CA_HEADS = 4
CA_DIM = 128
CA_WIDTH = CA_HEADS * CA_DIM
N_BRANCH = 3
D_FF = 2816
CONV_W = 3
EPS = 1e-6
F_FLOOR = 1e-12
MASK_NEG = -1e30
IN_SIZES = (HG_WIDTH,) * 5 + (NA_WIDTH,) * 3 + (CA_WIDTH,) + (D_MODEL,) * N_BRANCH
IN_COLS = sum(IN_SIZES)

kernel_name = 'hybrid_hgrn2_natten_memxattn_convffn'


def rmsnorm(x, g):
    xf = x.astype(jnp.float32)
    y = xf * lax.rsqrt(jnp.mean(xf * xf, axis=-1, keepdims=True) + EPS)
    return (y * g.astype(jnp.float32)).astype(x.dtype)


def to_heads(a, n):
    b, t, w = a.shape
    return a.reshape(b, t, n, w // n).transpose(0, 2, 1, 3)


def split_cols(a):
    return jnp.split(a, np.cumsum(IN_SIZES)[:-1].tolist(), axis=-1)


def gla_chunk_scan(q, k, v, log_f):
    b, h, t, dk = q.shape
    dv = v.shape[-1]
    n = t // HG_CHUNK

    def chunks(a):
        return jnp.moveaxis(a.reshape(b, h, n, HG_CHUNK, a.shape[-1]), 2, 0)

    tri = jnp.tril(jnp.ones((HG_CHUNK, HG_CHUNK), dtype=bool))[:, :, None]

    def step(S, inp):
        qc, kc, vc, gc = inp
        cum = jnp.cumsum(gc, axis=2)
        diff = cum[:, :, :, None, :] - cum[:, :, None, :, :]
        decay = jnp.where(tri, jnp.exp(jnp.where(tri, diff, 0.0)), 0.0)
        scores = jnp.einsum('bhtk,bhsk,bhtsk->bhts', qc, kc, decay)
        o = jnp.einsum('bhts,bhsv->bhtv', scores, vc) + jnp.einsum('bhtk,bhkv->bhtv', qc * jnp.exp(cum), S)
        last = cum[:, :, -1:, :]
        S = jnp.exp(last[:, :, 0, :, None]) * S + jnp.einsum('bhsk,bhsv->bhkv', kc * jnp.exp(last - cum), vc)
        return S, o

    S0 = jnp.zeros((b, h, dk, dv), q.dtype)
    _, o = lax.scan(step, S0, (chunks(q), chunks(k), chunks(v), chunks(log_f)))
    return jnp.moveaxis(o, 0, 2).reshape(b, h, t, dv)


def hgrn2_branch(zq, zi, zf_fwd, zf_bwd, zg, lb, gnorm, w_o):
    f32 = jnp.float32
    dt = zq.dtype
    b, t = zq.shape[:2]
    q = to_heads(jax.nn.silu(zq.astype(f32)) * HG_DIM ** -0.5, HG_HEADS)
    v = to_heads(zi.astype(f32), HG_HEADS)

    def one_direction(zf, lb_d, reverse):
        zf = zf.astype(f32)
        f = lb_d + (1.0 - lb_d) * jax.nn.sigmoid(zf)
        log_f = jnp.log(jnp.maximum(f, F_FLOOR))
        k = (1.0 - lb_d) * jax.nn.sigmoid(-zf)
        ins = (q, to_heads(k, HG_HEADS), v, to_heads(log_f, HG_HEADS))
        if reverse:
            ins = tuple(jnp.flip(a, axis=2) for a in ins)
            return jnp.flip(gla_chunk_scan(*ins), axis=2)
        return gla_chunk_scan(*ins)

    o = one_direction(zf_fwd, lb[0], False) + one_direction(zf_bwd, lb[1], True)
    o = rmsnorm(o.transpose(0, 2, 1, 3), gnorm)
    gate = jax.nn.silu(zg.astype(f32)).reshape(o.shape)
    o = (o * gate).reshape(b, t, HG_WIDTH).astype(dt)
    return o @ w_o


def neighbourhood_attention(zq, zk, zv, rpb, w_o):
    b, t, _ = zq.shape
    rows = t // GRID_W
    kh = min(NA_KH, rows)
    r = np.arange(rows)
    row_idx = np.clip(r - kh // 2, 0, rows - kh)[:, None] + np.arange(kh)[None, :]
    c = np.arange(GRID_W)
    col_start = np.clip(c - NA_KW // 2, 0, GRID_W - NA_KW)
    col_mask = (c[None, :] >= col_start[:, None]) & (c[None, :] < col_start[:, None] + NA_KW)
    dr = row_idx - r[:, None] + (NA_KH - 1)
    dc = np.clip(c[None, :] - c[:, None], -(NA_KW - 1), NA_KW - 1) + (NA_KW - 1)
    bias = rpb[:, dr[:, None, :, None], dc[None, :, None, :]]

    def grid(a):
        return a.reshape(b, rows, GRID_W, NA_HEADS, NA_DIM)

    q = grid(zq) * NA_DIM ** -0.5
    k = grid(zk)[:, row_idx]
    v = grid(zv)[:, row_idx]
    s = jnp.einsum('brqhd,brkwhd->bhrqkw', q, k).astype(jnp.float32) + bias.astype(jnp.float32)
    s = jnp.where(col_mask[:, None, :], s, MASK_NEG)
    p = jax.nn.softmax(s.reshape(b, NA_HEADS, rows, GRID_W, kh * GRID_W), axis=-1)
    p = p.reshape(s.shape).astype(zv.dtype)
    o = jnp.einsum('bhrqkw,brkwhd->brqhd', p, v).reshape(b, t, NA_WIDTH)
    return o @ w_o


def memory_cross_attention(zq, mem_n, w_kv, w_o):
    b, t, _ = zq.shape
    m = mem_n.shape[1]
    q = zq.reshape(b, t, CA_HEADS, CA_DIM) * CA_DIM ** -0.5
    kv = (mem_n @ w_kv).reshape(b, m, 2, CA_HEADS, CA_DIM)
    s = jnp.einsum('bthd,bmhd->bhtm', q, kv[:, :, 0]).astype(jnp.float32)
    p = jax.nn.softmax(s, axis=-1).astype(zq.dtype)
    o = jnp.einsum('bhtm,bmhd->bthd', p, kv[:, :, 1]).reshape(b, t, CA_WIDTH)
    return o @ w_o


def conv_ffn(h, w_up, conv_w, conv_b, w_down):
    t = h.shape[1]
    u = h @ w_up
    up = jnp.pad(u, ((0, 0), (CONV_W // 2, CONV_W // 2), (0, 0)))
    u = sum(up[:, j:j + t] * conv_w[j] for j in range(CONV_W)) + conv_b
    a, g = jnp.split(u, 2, axis=-1)
    return (jax.nn.gelu(a) * g) @ w_down


def setup_inputs(seed: int = 0) -> dict:
    key = jax.random.key(seed)
    ks = jax.random.split(key, 19)

    def nrm(k, shape, scale):
        return jax.random.normal(k, shape, jnp.float32) * scale

    return {
        'x': nrm(ks[0], (BATCH, SEQ, D_MODEL), 1.0),
        'mem': nrm(ks[1], (BATCH, MEM_LEN, D_MODEL), 1.0),
        'norm_mix': 1.0 + nrm(ks[2], (DEPTH, D_MODEL), 0.02),
        'w_in': nrm(ks[3], (DEPTH, D_MODEL, IN_COLS), D_MODEL ** -0.5),
        'hg_lb_logits': nrm(ks[4], (DEPTH, 2, HG_WIDTH), 1.0),
        'hg_gnorm': 1.0 + nrm(ks[5], (DEPTH, HG_DIM), 0.02),
        'w_hg_o': nrm(ks[6], (DEPTH, HG_WIDTH, D_MODEL), HG_WIDTH ** -0.5),
        'na_rpb': nrm(ks[7], (DEPTH, NA_HEADS, 2 * NA_KH - 1, 2 * NA_KW - 1), 0.1),
        'w_na_o': nrm(ks[8], (DEPTH, NA_WIDTH, D_MODEL), NA_WIDTH ** -0.5),
        'mem_norm': 1.0 + nrm(ks[9], (D_MODEL,), 0.02),
        'w_mem_kv': nrm(ks[10], (DEPTH, D_MODEL, 2 * CA_WIDTH), D_MODEL ** -0.5),
        'w_ca_o': nrm(ks[11], (DEPTH, CA_WIDTH, D_MODEL), CA_WIDTH ** -0.5),
        'w_out': nrm(ks[12], (DEPTH, D_MODEL, D_MODEL), D_MODEL ** -0.5),
        'norm_ffn': 1.0 + nrm(ks[13], (DEPTH, D_MODEL), 0.02),
        'w_up': nrm(ks[14], (DEPTH, D_MODEL, 2 * D_FF), D_MODEL ** -0.5),
        'conv_w': nrm(ks[15], (DEPTH, CONV_W, 2 * D_FF), CONV_W ** -0.5),
        'conv_b': nrm(ks[16], (DEPTH, 2 * D_FF), 0.01),
        'w_down': nrm(ks[17], (DEPTH, D_FF, D_MODEL), D_FF ** -0.5),
        'norm_final': 1.0 + nrm(ks[18], (D_MODEL,), 0.02),
    }


def reference(x, mem, norm_mix, w_in, hg_lb_logits, hg_gnorm, w_hg_o, na_rpb, w_na_o, mem_norm,
              w_mem_kv, w_ca_o, w_out, norm_ffn, w_up, conv_w, conv_b, w_down, norm_final):
    p_lb = jax.nn.softmax(hg_lb_logits.astype(jnp.float32), axis=0)
    lower_bounds = jnp.clip(jnp.cumsum(p_lb, axis=0) - p_lb[0], 0.0, 1.0)
    mem_n = rmsnorm(mem, mem_norm)
    for l in range(DEPTH):
        h = rmsnorm(x, norm_mix[l])
        (hq, hi, hff, hfb, hg, nq, nk, nv, cq, g_hg, g_na, g_ca) = split_cols(h @ w_in[l])
        y_hg = hgrn2_branch(hq, hi, hff, hfb, hg, lower_bounds[l], hg_gnorm[l], w_hg_o[l])
        y_na = neighbourhood_attention(nq, nk, nv, na_rpb[l], w_na_o[l])
        y_ca = memory_cross_attention(cq, mem_n, w_mem_kv[l], w_ca_o[l])
        merged = jax.nn.sigmoid(g_hg) * y_hg + jax.nn.sigmoid(g_na) * y_na + jax.nn.sigmoid(g_ca) * y_ca
        x = x + merged @ w_out[l]
        x = x + conv_ffn(rmsnorm(x, norm_ffn[l]), w_up[l], conv_w[l], conv_b[l], w_down[l])
    return rmsnorm(x, norm_final)
```

```python
import numpy as np
from contextlib import ExitStack

import concourse.bass as bass
import concourse.mybir as mybir
from concourse.bass_utils import run_bass_kernel_spmd

F32 = mybir.dt.float32
BF16 = mybir.dt.bfloat16
AF = mybir.ActivationFunctionType
ALU = mybir.AluOpType

D = 1024
T = 2048
NT = T // 128
NTT = T // 512
DEPTH = 2
IN_COLS = 10240
D_FF = 2816
NFF = D_FF // 128
EPS = 1e-6
LOGF_FLOOR = float(np.log(1e-12))
NEG = -30000.0
NSLOT = 47
EPOCH = 12000
DEBUG = False

C_HQ, C_HI, C_HFF, C_HFB, C_HG = 0, 1024, 2048, 3072, 4096
C_NQ, C_NK, C_NV, C_CQ = 5120, 5632, 6144, 6656
C_GHG, C_GNA, C_GCA = 7168, 8192, 9216


class _Rec:
    def __init__(self):
        self.call = None

    def __getattr__(self, name):
        def f(*a, **k):
            self.call = (name, a, k)
            return self
        return f


class Tok:
    __slots__ = ("w", "r", "rd")

    def __init__(self):
        self.w = None
        self.r = {}
        self.rd = []


class Eng:
    def __init__(self, fw, idx, name, nsem):
        self.fw, self.idx, self.name = fw, idx, name
        self.n = 0
        self.seen = {}
        self.prog = []
        self.sems = [fw.ctx.enter_context(fw.nc.semaphore(f"s_{name}_{i}")) for i in range(nsem)]

    def wait(self, d):
        if d is None:
            return
        key = d[:2]
        val = d[2]
        if self.seen.get(key, 0) >= val:
            return
        self.seen[key] = val
        if d[0] == "E":
            e = self.fw.engs[d[1]]
            ep = (val - 1) // EPOCH
            sem, v = e.sems[ep], val - ep * EPOCH
        else:
            sem, v = self.fw.dsems[d[1]], val
        self.prog.append(lambda h, sem=sem, v=v: h.wait_ge(sem, v))


class FW:
    def __init__(self, nc, ctx, ndsem=32, nsem=10):
        self.nc, self.ctx = nc, ctx
        self.engs = []
        for name, ns in (("pe", nsem), ("act", nsem), ("dve", nsem), ("pool", nsem), ("sp", 1)):
            self.engs.append(Eng(self, len(self.engs), name, ns))
        self.pe, self.act, self.dve, self.pool, self.sp = self.engs
        self.dsems = [ctx.enter_context(nc.semaphore(f"s_dma_{i}")) for i in range(ndsem)]
        self.dval = [0] * ndsem
        half = ndsem // 2
        self.qsems = {self.sp.idx: list(range(0, half)), self.pool.idx: list(range(half, ndsem))}
        self.qnext = {self.sp.idx: 0, self.pool.idx: 0}

    def op(self, eng, fn, reads=(), writes=()):
        pe = eng is self.pe
        for t in reads:
            if t.w is not None and not (pe and t.w[0] == "E" and t.w[1] == eng.idx):
                eng.wait(t.w)
        for t in writes:
            if t.w is not None and not (pe and t.w[0] == "E" and t.w[1] == eng.idx):
                eng.wait(t.w)
            for ei, n in t.r.items():
                if pe and ei == eng.idx:
                    continue
                eng.wait(("E", ei, n))
            for d in t.rd:
                eng.wait(d)
        eng.n += 1
        assert eng.n <= EPOCH * len(eng.sems), f"too many instructions on {eng.name}"
        ep = (eng.n - 1) // EPOCH
        rec = _Rec()
        fn(rec)
        name, a, k = rec.call
        eng.prog.append(lambda h, name=name, a=a, k=k, sem=eng.sems[ep]: getattr(h, name)(*a, **k).then_inc(sem, 1))
        for t in reads:
            if t.r.get(eng.idx, 0) < eng.n:
                t.r[eng.idx] = eng.n
        me = ("E", eng.idx, eng.n)
        for t in writes:
            t.w = me
            t.r = {}
            t.rd = []

    def dma(self, q, out, in_, reads=(), writes=(), **kw):
        for t in reads:
            q.wait(t.w)
        for t in writes:
            q.wait(t.w)
            for ei, n in t.r.items():
                q.wait(("E", ei, n))
            for d in t.rd:
                q.wait(d)
        lst = self.qsems[q.idx]
        si = lst[self.qnext[q.idx]]
        self.qnext[q.idx] = (self.qnext[q.idx] + 1) % len(lst)
        if self.dval[si] > 0:
            q.wait(("D", si, self.dval[si]))
        self.dval[si] += 16
        q.prog.append(lambda h, out=out, in_=in_, kw=kw, sem=self.dsems[si]:
                      h.dma_start(out=out, in_=in_, **kw).then_inc(sem, 16))
        me = ("D", si, self.dval[si])
        for t in reads:
            t.rd.append(me)
            if len(t.rd) > 64:
                t.rd = t.rd[-64:]
        for t in writes:
            t.w = me
            t.r = {}
            t.rd = []

    def allgather(self, cin, cout, groups):
        self.barrier()
        sem = self.ctx.enter_context(self.nc.semaphore(f"s_cc_{len(self.dsems)}"))
        self.dsems.append(sem)
        self.dval.append(1)
        self.pool.prog.append(lambda h, sem=sem: h.collective_compute("AllGather", ALU.bypass, ins=[cin], outs=[cout],
                                                                     replica_groups=groups).then_inc(sem))
        return ("D", len(self.dsems) - 1, 1)

    def barrier(self):
        deps = [("E", e.idx, e.n) for e in self.engs[:4] if e.n > 0]
        deps += [("D", i, v) for i, v in enumerate(self.dval) if v > 0]
        for e in self.engs:
            for d in deps:
                if not (d[0] == "E" and d[1] == e.idx):
                    e.wait(d)

    def finish(self):
        for i, v in enumerate(self.dval):
            if v > 0:
                self.sp.wait(("D", i, v))

    def emit(self, block):
        def run(eng):
            def f(h):
                for p in eng.prog:
                    p(h)
            return f
        block.tensor(run(self.pe))
        block.scalar(run(self.act))
        block.vector(run(self.dve))
        block.gpsimd(run(self.pool))
        block.sync(run(self.sp))


class Arena:
    def __init__(self, nc, ctx, fw, words):
        self.t = ctx.enter_context(nc.sbuf_tensor("arena", [128, words], F32))
        self.words, self.off, self.fw = words, 0, fw

    def alloc(self, shape, dt):
        n = int(np.prod(shape))
        w = n if dt == F32 else (n + 1) // 2
        w = (w + 7) // 8 * 8
        assert self.off + w <= self.words, f"SBUF arena overflow: {self.off}+{w}>{self.words}"
        ap = self.t[:, self.off:self.off + w]
        self.off += w
        if dt != F32:
            ap = ap.bitcast(dt)
            ap = ap[:, :n]
        else:
            ap = ap[:, :n]
        if len(shape) == 2:
            ap = ap.rearrange("p (a b) -> p a b", a=shape[0])
        elif len(shape) == 3:
            ap = ap.rearrange("p (a b c) -> p a b c", a=shape[0], b=shape[1])
        return ap, Tok()

    def mark(self):
        return self.off

    def release(self, m):
        self.fw.barrier()
        self.off = m


class Ring:
    def __init__(self, items):
        self.items, self.i = items, 0

    def next(self):
        it = self.items[self.i]
        self.i = (self.i + 1) % len(self.items)
        return it


class Ctx:
    pass


def setup(nc, ctx, arena_words):
    c = Ctx()
    c.nc = nc
    c.fw = FW(nc, ctx)
    c.A = Arena(nc, ctx, c.fw, arena_words)
    banks = []
    for i in range(6):
        t = ctx.enter_context(nc.psum_tensor(f"psf{i}", [128, 512], F32))
        banks.append((t, Tok()))
    c.ps = Ring(banks[:4])
    c.pacc = banks[4:]
    bb = []
    for i in range(2):
        t = ctx.enter_context(nc.psum_tensor(f"psb{i}", [128, 8, 128], BF16))
        bb.append((t, Tok()))
    c.psb = Ring(bb)
    return c


def load_consts(c, ident_d, tri_d):
    fw, A = c.fw, c.A
    c.ident, c.t_ident = A.alloc([128], BF16)
    fw.dma(fw.pool, c.ident, ident_d[:, :], writes=[c.t_ident])
    c.tri, c.t_tri = A.alloc([2, 64], BF16)
    fw.dma(fw.pool, c.tri, tri_d[:, :, :], writes=[c.t_tri])
    c.ones, c.t_ones = A.alloc([128], BF16)
    fw.op(fw.dve, lambda e: e.memset(c.ones, 1.0), writes=[c.t_ones])
    c.onesd, c.t_onesd = A.alloc([128], BF16)
    fw.op(fw.dve, lambda e: e.memset(c.onesd, 1.0 / 128), writes=[c.t_onesd])


def load_vecT(c, vec_d, n):
    ap, tok = c.A.alloc([n], F32)
    c.fw.dma(c.fw.sp, ap, vec_d.rearrange("(c p) -> p c", p=128), writes=[tok], allow_slow_non_contiguous=True)
    return ap, tok


def dram_loader(c, x_d, rows=128):
    def load(i, xt, t_x):
        c.fw.dma(c.fw.sp, xt[:rows], x_d[i * rows:(i + 1) * rows, :], writes=[t_x])
    return load


def rmsnorm_T(c, load, ntiles, gT, t_g, hT, t_h, col0=0, rows=128):
    fw, A = c.fw, c.A
    m = A.mark()
    nb = 4
    xr = Ring([A.alloc([1024], F32) for _ in range(nb)])
    xnr = Ring([A.alloc([1024], BF16) for _ in range(2)])
    junk, t_junk = A.alloc([1024], BF16)
    ssr = Ring([A.alloc([1], F32) for _ in range(nb)])

    def stage_a(i):
        xt, t_x = xr.next()
        load(i, xt, t_x)
        ss, t_ss = ssr.next()
        fw.op(fw.act, lambda e: e.activation(out=junk[:rows], in_=xt[:rows], func=AF.Square, accum_out=ss[:rows]),
              reads=[t_x], writes=[t_junk, t_ss])
        fw.op(fw.act, lambda e: e.activation(out=ss[:rows], in_=ss[:rows], func=AF.Sqrt, scale=1.0 / 1024, bias=EPS),
              reads=[t_ss], writes=[t_ss])
        fw.op(fw.dve, lambda e: e.reciprocal(out=ss[:rows], in_=ss[:rows]), reads=[t_ss], writes=[t_ss])
        return xt, t_x, ss, t_ss

    def stage_b(i, xt, t_x, ss, t_ss):
        xn, t_xn = xnr.next()
        fw.op(fw.act, lambda e: e.activation(out=xn[:rows], in_=xt[:rows], func=AF.Copy, scale=ss[:rows]),
              reads=[t_x, t_ss], writes=[t_xn])
        pb, t_pb = c.psb.next()
        for kc in range(8):
            fw.op(fw.pe, lambda e: e.transpose(out=pb[:, kc, :rows], in_=xn[:rows, kc * 128:(kc + 1) * 128], identity=c.ident[:rows, :rows]),
                  reads=[t_xn, c.t_ident], writes=[t_pb])
        fw.op(fw.dve, lambda e: e.tensor_tensor(out=hT[:, :, col0 + i * rows: col0 + (i + 1) * rows], in0=pb[:, :, :rows],
                                                in1=gT.unsqueeze(2).broadcast_to([128, 8, rows]), op=ALU.mult),
              reads=[t_pb, t_g], writes=[t_h])

    pend = [stage_a(0)]
    if ntiles > 1:
        pend.append(stage_a(1))
    for i in range(ntiles):
        if i + 2 < ntiles:
            pend.append(stage_a(i + 2))
        stage_b(i, *pend.pop(0))
    A.release(m)


def norm_scratch(c):
    A = c.A
    NS = Ctx()
    NS.junk, NS.t_junk = A.alloc([1024], BF16)
    NS.ssr = Ring([A.alloc([1], F32) for _ in range(4)])
    NS.xnr = Ring([A.alloc([1024], BF16) for _ in range(3)])
    return NS


def norm_tile_a(c, NS, xt, t_x):
    fw = c.fw
    ss, t_ss = NS.ssr.next()
    fw.op(fw.act, lambda e: e.activation(out=NS.junk, in_=xt, func=AF.Square, accum_out=ss), reads=[t_x], writes=[NS.t_junk, t_ss])
    fw.op(fw.act, lambda e: e.activation(out=ss, in_=ss, func=AF.Sqrt, scale=1.0 / 1024, bias=EPS), reads=[t_ss], writes=[t_ss])
    fw.op(fw.dve, lambda e: e.reciprocal(out=ss, in_=ss), reads=[t_ss], writes=[t_ss])
    xn, t_xn = NS.xnr.next()
    fw.op(fw.act, lambda e: e.activation(out=xn, in_=xt, func=AF.Copy, scale=ss), reads=[t_x, t_ss], writes=[t_xn])
    return xn, t_xn


def norm_tile_b(c, i, xn, t_xn, gT, t_g, hT, t_h):
    fw = c.fw
    pb, t_pb = c.psb.next()
    for kc in range(8):
        fw.op(fw.pe, lambda e: e.transpose(out=pb[:, kc, :], in_=xn[:, kc * 128:(kc + 1) * 128], identity=c.ident), reads=[t_xn, c.t_ident], writes=[t_pb])
    fw.op(fw.dve, lambda e: e.tensor_tensor(out=hT[:, :, i * 128:(i + 1) * 128], in0=pb[:, :, :], in1=gT.unsqueeze(2).broadcast_to([128, 8, 128]), op=ALU.mult),
          reads=[t_pb, t_g], writes=[t_h])


def wsrc_cols(w_d, col0, ncols):
    return w_d[:, col0:col0 + ncols].rearrange("(kc p) n -> p kc n", p=128)


def proj_fm(c, wr, srcs, actT, t_act, KC, ntt, evac, extra=None):
    fw = c.fw
    for ci, src in enumerate(srcs):
        wb, t_wb = wr.next()
        fw.dma(fw.pool, wb[:, :KC, :], src, writes=[t_wb])
        for tt in range(ntt):
            ps, t_ps = c.ps.next()
            for kc in range(KC):
                fw.op(fw.pe, lambda e, kc=kc, wb=wb, ps=ps, tt=tt: e.matmul(ps[:, :], lhsT=wb[:, kc, :], rhs=actT[:, kc, tt * 512:(tt + 1) * 512],
                                                                           start=(kc == 0), stop=(kc == KC - 1)),
                      reads=[t_wb, t_act], writes=[t_ps])
            evac(ci, tt, ps, t_ps)
        if extra is not None:
            extra(ci, wb, t_wb)


def proj_tm(c, wb, t_wb, ncols, actT, t_act, KC, tiles, evac):
    fw = c.fw
    gsz = 512 // ncols
    for g0 in range(0, len(tiles), gsz):
        grp = tiles[g0:g0 + gsz]
        ps, t_ps = c.ps.next()
        for j, tk in enumerate(grp):
            for kc in range(KC):
                fw.op(fw.pe, lambda e, j=j, tk=tk, kc=kc, ps=ps: e.matmul(ps[:, j * ncols:(j + 1) * ncols], lhsT=actT[:, kc, tk:tk + 128],
                                                                         rhs=wb[:, kc, :ncols], start=(kc == 0), stop=(kc == KC - 1)),
                      reads=[t_wb, t_act], writes=[t_ps])
        evac(g0, len(grp), ps, t_ps)


def hgrn_phase(c, P, hT, t_h, w_in_d, own, S_par, og_d=None):
    fw, A = c.fw, c.A
    B = Ctx()
    B.wr = Ring([A.alloc([8, 128], BF16) for _ in range(3)])
    pb = []
    sgf = A.alloc([T], F32)
    for _ in range(2):
        X = Ctx()
        if own:
            X.qs, X.t_qs = A.alloc([T], BF16)
            X.sgate, X.t_sgate = A.alloc([T], BF16)
        X.sg = [sgf, A.alloc([T], F32)]
        pb.append(X)
    B.v, B.t_v = A.alloc([NT, 128], BF16)
    B.lf, B.t_lf = A.alloc([T], F32)
    B.cum, B.t_cum = A.alloc([T], F32)
    B.tot, B.t_tot = A.alloc([32], F32)
    B.kk, B.t_kk = A.alloc([T], BF16)
    B.e = [A.alloc([T], BF16) for _ in range(2 if own else 1)]
    sb = []
    if own:
        Kt1 = A.alloc([T], BF16)
    Kh1 = A.alloc([T], BF16)
    for _ in range(2):
        X = Ctx()
        if own:
            X.Kt, X.t_Kt = Kt1
            X.Qh, X.t_Qh = A.alloc([T], BF16)
            X.Asb, X.t_Asb = A.alloc([16, 64], BF16)
        X.Kh, X.t_Kh = Kh1
        X.elast, X.t_elast = A.alloc([32], F32)
        sb.append(X)
    for X in sb:
        X.Pn, X.t_Pn = A.alloc([32, 128], BF16)
    B.Sall, _t = A.alloc([33, 128], BF16)
    B.Stoks = [Tok() for _ in range(33)]
    if own:
        B.oacc, B.t_oacc = A.alloc([T], F32)
        B.sq = Ring([A.alloc([512], BF16) for _ in range(2)])
        B.rs = Ring([A.alloc([512], F32) for _ in range(2)])
        B.og, B.t_og = A.alloc([T], BF16)
    B.KT = Ring([A.alloc([4, 128], BF16) for _ in range(4)])
    B.seg, B.t_seg = A.alloc([T], BF16)
    fw.op(fw.dve, lambda e: e.memset(B.seg, 1.0), writes=[B.t_seg])
    fw.op(fw.dve, lambda e: e.memset(B.seg.rearrange("p (n c) -> p n c", c=64)[:, :, 0:1], 0.0), writes=[B.t_seg])

    def proj(h, names):
        X = pb[h % 2]
        cols = {"q": C_HQ, "g": C_HG, "ff": C_HFF, "fb": C_HFB}
        fm = [n for n in names if n != "v"]

        def evac(ci, tt, ps, t_ps):
            name = fm[ci]
            sl = slice(tt * 512, (tt + 1) * 512)
            if name == "q":
                fw.op(fw.act, lambda e: e.activation(out=X.qs[:, sl], in_=ps[:, :], func=AF.Silu), reads=[t_ps], writes=[X.t_qs])
            elif name == "g":
                fw.op(fw.act, lambda e: e.activation(out=X.sgate[:, sl], in_=ps[:, :], func=AF.Silu), reads=[t_ps], writes=[X.t_sgate])
            else:
                d = 0 if name == "ff" else 1
                fw.op(fw.act, lambda e: e.activation(out=X.sg[d][0][:, sl], in_=ps[:, :], func=AF.Sigmoid), reads=[t_ps], writes=[X.sg[d][1]])
        if fm:
            proj_fm(c, B.wr, [wsrc_cols(w_in_d, cols[n] + h * 128, 128) for n in fm], hT, t_h, 8, NTT, evac)
        if "v" in names:
            wb, t_wb = B.wr.next()
            fw.dma(fw.pool, wb, wsrc_cols(w_in_d, C_HI + h * 128, 128), writes=[t_wb])

            def evac_v(g0, n, ps, t_ps):
                fw.op(fw.act, lambda e: e.activation(out=B.v[:, g0:g0 + n, :], in_=ps[:, :n * 128].rearrange("p (a b) -> p a b", b=128), func=AF.Copy),
                      reads=[t_ps], writes=[B.t_v])
            proj_tm(c, wb, t_wb, 128, hT, t_h, 8, [i * 128 for i in range(NT)], evac_v)

    def elem(h, d):
        X = pb[h % 2]
        Y = sb[d]
        sg, t_sg = X.sg[d]
        lb = P.lb[:, d, h:h + 1]
        oml = P.oml[:, d, h:h + 1]
        noml = P.noml[:, d, h:h + 1]
        fw.op(fw.act, lambda e: e.activation(out=B.lf, in_=sg, func=AF.Ln, scale=oml, bias=lb), reads=[t_sg, P.t_lb], writes=[B.t_lf])
        fw.op(fw.act, lambda e: e.activation(out=B.kk, in_=sg, func=AF.Identity, scale=noml, bias=oml), reads=[t_sg, P.t_lb], writes=[B.t_kk])
        fw.op(fw.dve, lambda e: e.tensor_scalar(out=B.lf, in0=B.lf, scalar1=LOGF_FLOOR, scalar2=None, op0=ALU.max), reads=[B.t_lf], writes=[B.t_lf])
        fw.op(fw.dve, lambda e: e.tensor_tensor_scan(out=B.cum, data0=B.seg, data1=B.lf, initial=0.0, op0=ALU.mult, op1=ALU.add),
              reads=[B.t_seg, B.t_lf], writes=[B.t_cum])
        v3 = lambda ap: ap.rearrange("p (n c) -> p n c", c=64)
        cumb, t_cumb = B.cum, B.t_cum
        if d == 1:
            fw.op(fw.act, lambda e: e.activation(out=B.tot, in_=v3(B.cum)[:, :, 63], func=AF.Copy), reads=[B.t_cum], writes=[B.t_tot])
            fw.op(fw.dve, lambda e: e.tensor_tensor(out=B.cum, in0=B.lf, in1=B.cum, op=ALU.subtract), reads=[B.t_lf, B.t_cum, B.t_tot], writes=[B.t_cum])
            fw.op(fw.dve, lambda e: e.tensor_tensor(out=v3(B.lf), in0=v3(B.cum), in1=B.tot.unsqueeze(2).broadcast_to([128, 32, 64]), op=ALU.add),
                  reads=[B.t_cum, B.t_tot], writes=[B.t_lf])
            cumb, t_cumb = B.lf, B.t_lf
        c3 = v3(cumb)
        ilast = 63 if d == 0 else 0
        fw.op(fw.act, lambda e: e.activation(out=Y.elast, in_=c3[:, :, ilast], func=AF.Exp), reads=[t_cumb], writes=[Y.t_elast])
        en, t_en = B.e[0]
        fw.op(fw.act, lambda e: e.activation(out=en, in_=cumb, func=AF.Exp, scale=-1.0), reads=[t_cumb], writes=[t_en])
        if own:
            ep, t_ep = B.e[1]
            fw.op(fw.act, lambda e: e.activation(out=ep, in_=cumb, func=AF.Exp), reads=[t_cumb], writes=[t_ep])
            fw.op(fw.dve, lambda e: e.tensor_tensor(out=Y.Kt, in0=B.kk, in1=en, op=ALU.mult), reads=[B.t_kk, t_en], writes=[Y.t_Kt])
            fw.op(fw.dve, lambda e: e.tensor_tensor(out=v3(Y.Kh), in0=v3(Y.Kt), in1=Y.elast.unsqueeze(2).broadcast_to([128, 32, 64]), op=ALU.mult),
                  reads=[Y.t_Kt, Y.t_elast], writes=[Y.t_Kh])
            fw.op(fw.dve, lambda e: e.scalar_tensor_tensor(out=Y.Qh, in0=X.qs, scalar=128 ** -0.5, in1=ep, op0=ALU.mult, op1=ALU.mult),
                  reads=[X.t_qs, t_ep], writes=[Y.t_Qh])
        else:
            fw.op(fw.dve, lambda e: e.tensor_tensor(out=v3(B.tmp), in0=c3, in1=c3[:, :, ilast:ilast + 1].broadcast_to([128, 32, 64]), op=ALU.subtract),
                  reads=[t_cumb], writes=[B.t_tmp])
            fw.op(fw.act, lambda e: e.activation(out=en, in_=B.tmp, func=AF.Exp, scale=-1.0), reads=[B.t_tmp], writes=[t_en])
            fw.op(fw.dve, lambda e: e.tensor_tensor(out=Y.Kh, in0=B.kk, in1=en, op=ALU.mult), reads=[B.t_kk, t_en], writes=[Y.t_Kh])

    def stage1(h, d):
        Y = sb[d]
        kts = []
        for g in range(4):
            n0 = 8 * g
            if own:
                psA, t_psA = c.ps.next()
                pA = psA[:, 0:256].rearrange("p (a b) -> p a b", b=64)
                for j in range(8):
                    n = n0 + j
                    pr, slot = j % 2, j // 2
                    fw.op(fw.pe, lambda e: e.matmul(pA[pr * 64:(pr + 1) * 64, slot, :], lhsT=Y.Kt[:, n * 64:(n + 1) * 64],
                                                    rhs=Y.Qh[:, n * 64:(n + 1) * 64], start=True, stop=True),
                          reads=[Y.t_Kt, Y.t_Qh], writes=[t_psA])
                fw.op(fw.dve, lambda e: e.tensor_tensor(out=Y.Asb[:, 4 * g:4 * g + 4, :], in0=pA, in1=c.tri[:, d, :].unsqueeze(1).broadcast_to([128, 4, 64]), op=ALU.mult),
                      reads=[t_psA, c.t_tri], writes=[Y.t_Asb])
            pb_, t_pb = c.psb.next()
            for slot in range(4):
                tk = (n0 + 2 * slot) * 64
                fw.op(fw.pe, lambda e: e.transpose(out=pb_[:, slot, :], in_=Y.Kh[:, tk:tk + 128], identity=c.ident),
                      reads=[Y.t_Kh, c.t_ident], writes=[t_pb])
            KT, t_KT = B.KT.next()
            fw.op(fw.act, lambda e: e.activation(out=KT, in_=pb_[:, 0:4, :], func=AF.Copy), reads=[t_pb], writes=[t_KT])
            kts.append((KT, t_KT))
        for g in range(4):
            n0 = 8 * g
            KT, t_KT = kts[g]
            psP = [c.ps.next(), c.ps.next()]
            for j in range(8):
                n = n0 + j
                pr, slot = j % 2, j // 2
                pp, t_pp = psP[pr]
                fw.op(fw.pe, lambda e: e.matmul(pp[:, slot * 128:(slot + 1) * 128], lhsT=KT[pr * 64:(pr + 1) * 64, slot, :],
                                                rhs=B.v[pr * 64:(pr + 1) * 64, n // 2, :], start=True, stop=True),
                      reads=[t_KT, B.t_v], writes=[t_pp])
            pn4 = Y.Pn[:, n0:n0 + 8, :].rearrange("p (a two) b -> p a two b", two=2)
            for pr in range(2):
                pp, t_pp = psP[pr]
                fw.op(fw.act, lambda e: e.activation(out=pn4[:, :, pr, :], in_=pp[:, :].rearrange("p (a b) -> p a b", b=128), func=AF.Copy),
                      reads=[t_pp], writes=[Y.t_Pn])

    def stage2(h, d):
        Y = sb[d]
        if own:
            s_ap, s_tok = S_par[d][0][:, h, :], S_par[d][1]
            fw.op(fw.dve, lambda e: e.tensor_copy(out=B.Sall[:, 0, :], in_=s_ap), reads=[s_tok], writes=[B.Stoks[0]])
        else:
            fw.op(fw.dve, lambda e: e.memset(B.Sall[:, 0, :], 0.0), writes=[B.Stoks[0]])
        for i in range(32):
            n = i if d == 0 else 31 - i
            fw.op(fw.dve, lambda e: e.scalar_tensor_tensor(out=B.Sall[:, i + 1, :], in0=B.Sall[:, i, :], scalar=Y.elast[:, n:n + 1], in1=Y.Pn[:, n, :],
                                                           op0=ALU.mult, op1=ALU.add),
                  reads=[B.Stoks[i], Y.t_elast, Y.t_Pn], writes=[B.Stoks[i + 1]])
        if not own:
            so_ap, so_tok = S_par[d][0][:, h, :], S_par[d][1]
            fw.op(fw.dve, lambda e: e.tensor_scalar(out=so_ap, in0=B.Sall[:, 32, :], scalar1=P.sel[:, d:d + 1], scalar2=None, op0=ALU.mult),
                  reads=[B.Stoks[32], P.t_sel], writes=[so_tok])

    def stage3(h, d):
        Y = sb[d]
        for g in range(4):
            n0 = 8 * g
            psO = [c.ps.next(), c.ps.next()]
            for j in range(8):
                n = n0 + j
                i = n if d == 0 else 31 - n
                pr, slot = j % 2, j // 2
                po, t_po = psO[pr]
                fw.op(fw.pe, lambda e: e.matmul(po[:, slot * 64:(slot + 1) * 64], lhsT=B.v[pr * 64:(pr + 1) * 64, n // 2, :],
                                                rhs=Y.Asb[pr * 64:(pr + 1) * 64, n // 2, :], start=True, stop=False),
                      reads=[B.t_v, Y.t_Asb], writes=[t_po])
                fw.op(fw.pe, lambda e: e.matmul(po[:, slot * 64:(slot + 1) * 64], lhsT=B.Sall[:, i, :], rhs=Y.Qh[:, n * 64:(n + 1) * 64], start=False, stop=True),
                      reads=[B.Stoks[i], Y.t_Qh], writes=[t_po])
            oa = B.oacc[:, n0 * 64:(n0 + 8) * 64].rearrange("p (a two b) -> p a two b", two=2, b=64)
            for pr in range(2):
                po, t_po = psO[pr]
                src = po[:, 0:256].rearrange("p (a b) -> p a b", b=64)
                if d == 0:
                    fw.op(fw.act, lambda e: e.activation(out=oa[:, :, pr, :], in_=src, func=AF.Copy), reads=[t_po], writes=[B.t_oacc])
                else:
                    fw.op(fw.dve, lambda e: e.tensor_tensor(out=oa[:, :, pr, :], in0=oa[:, :, pr, :], in1=src, op=ALU.add),
                          reads=[t_po, B.t_oacc], writes=[B.t_oacc])

    def norm(h):
        X = pb[h % 2]
        for tt in range(NTT):
            sl = slice(tt * 512, (tt + 1) * 512)
            sq, t_sq = B.sq.next()
            fw.op(fw.act, lambda e: e.activation(out=sq, in_=B.oacc[:, sl], func=AF.Square), reads=[B.t_oacc], writes=[t_sq])
            ps, t_ps = c.ps.next()
            fw.op(fw.pe, lambda e: e.matmul(ps[:, :], lhsT=c.onesd, rhs=sq, start=True, stop=True), reads=[c.t_onesd, t_sq], writes=[t_ps])
            rs, t_rs = B.rs.next()
            fw.op(fw.act, lambda e: e.activation(out=rs, in_=ps[:, :], func=AF.Ln, bias=P.epsc[:, 0:1]), reads=[t_ps, P.t_epsc], writes=[t_rs])
            fw.op(fw.act, lambda e: e.activation(out=rs, in_=rs, func=AF.Exp, scale=-0.5), reads=[t_rs], writes=[t_rs])
            fw.op(fw.dve, lambda e: e.tensor_tensor(out=rs, in0=rs, in1=B.oacc[:, sl], op=ALU.mult), reads=[t_rs, B.t_oacc], writes=[t_rs])
            fw.op(fw.dve, lambda e: e.scalar_tensor_tensor(out=B.og[:, sl], in0=rs, scalar=P.gnorm[:, 0:1], in1=X.sgate[:, sl], op0=ALU.mult, op1=ALU.mult),
                  reads=[t_rs, P.t_gnorm, X.t_sgate], writes=[B.t_og])
        fw.dma(fw.sp, og_d[h * 128:(h + 1) * 128, :], B.og, reads=[B.t_og])

    nxt = lambda h, names: proj(h + 1, [n for n in names if own or n in ("ff", "fb")]) if h + 1 < 8 else None
    proj(0, ["q", "ff", "fb", "g"])
    for h in range(8):
        elem(h, 0)
        proj(h, ["v"])
        if h > 0:
            norm(h - 1)
        nxt(h, ["q"])
        stage1(h, 0)
        elem(h, 1)
        nxt(h, ["ff"])
        stage1(h, 1)
        stage2(h, 0)
        nxt(h, ["fb"])
        stage3(h, 0)
        stage2(h, 1)
        nxt(h, ["g"])
        stage3(h, 1)
    norm(7)


def hgrn_partner(c, P, hT, t_h, w_in_d, S_par):
    fw, A = c.fw, c.A
    wr = Ring([A.alloc([8, 128], BF16) for _ in range(3)])
    sgr = [Ring([A.alloc([T], F32) for _ in range(2)]) for _ in range(2)]
    vr = Ring([A.alloc([NT, 128], BF16) for _ in range(2)])
    lfr = Ring([A.alloc([T], F32) for _ in range(2)])
    cumr = Ring([A.alloc([T], F32) for _ in range(2)])
    kkr = Ring([A.alloc([T], BF16) for _ in range(2)])
    er = Ring([A.alloc([T], BF16) for _ in range(2)])
    Khr = Ring([A.alloc([T], BF16) for _ in range(2)])
    KTr = Ring([A.alloc([4, 128], BF16) for _ in range(4)])
    onesT, t_onesT = A.alloc([T], BF16)
    fw.op(fw.dve, lambda e: e.memset(onesT, 1.0), writes=[t_onesT])

    def proj_gate(h, d):
        sg, t_sg = sgr[d].next()

        def evac(ci, tt, ps, t_ps):
            fw.op(fw.act, lambda e: e.activation(out=sg[:, tt * 512:(tt + 1) * 512], in_=ps[:, :], func=AF.Sigmoid), reads=[t_ps], writes=[t_sg])
        proj_fm(c, wr, [wsrc_cols(w_in_d, (C_HFF if d == 0 else C_HFB) + h * 128, 128)], hT, t_h, 8, NTT, evac)
        return sg, t_sg

    def proj_v(h):
        wb, t_wb = wr.next()
        fw.dma(fw.pool, wb, wsrc_cols(w_in_d, C_HI + h * 128, 128), writes=[t_wb])
        v, t_v = vr.next()

        def evac_v(g0, n, ps, t_ps):
            fw.op(fw.act, lambda e: e.activation(out=v[:, g0:g0 + n, :], in_=ps[:, :n * 128].rearrange("p (a b) -> p a b", b=128), func=AF.Copy),
                  reads=[t_ps], writes=[t_v])
        proj_tm(c, wb, t_wb, 128, hT, t_h, 8, [i * 128 for i in range(NT)], evac_v)
        return v, t_v

    def st_elem(h, d, sg, t_sg):
        lb = P.lb[:, d, h:h + 1]
        oml = P.oml[:, d, h:h + 1]
        noml = P.noml[:, d, h:h + 1]
        lf, t_lf = lfr.next()
        cum, t_cum = cumr.next()
        kk, t_kk = kkr.next()
        ex, t_ex = er.next()
        Kh, t_Kh = Khr.next()
        fw.op(fw.act, lambda e: e.activation(out=lf, in_=sg, func=AF.Ln, scale=oml, bias=lb), reads=[t_sg, P.t_lb], writes=[t_lf])
        fw.op(fw.act, lambda e: e.activation(out=kk, in_=sg, func=AF.Identity, scale=noml, bias=oml), reads=[t_sg, P.t_lb], writes=[t_kk])
        fw.op(fw.dve, lambda e: e.tensor_scalar(out=lf, in0=lf, scalar1=LOGF_FLOOR, scalar2=None, op0=ALU.max), reads=[t_lf], writes=[t_lf])
        fw.op(fw.dve, lambda e: e.tensor_tensor_scan(out=cum, data0=onesT, data1=lf, initial=0.0, op0=ALU.mult, op1=ALU.add),
              reads=[t_onesT, t_lf], writes=[t_cum])
        if d == 0:
            fw.op(fw.act, lambda e: e.activation(out=ex, in_=cum, func=AF.Exp, scale=-1.0, bias=cum[:, T - 1:T]), reads=[t_cum], writes=[t_ex])
        else:
            fw.op(fw.dve, lambda e: e.tensor_tensor(out=lf, in0=cum, in1=lf, op=ALU.subtract), reads=[t_cum, t_lf], writes=[t_lf])
            fw.op(fw.act, lambda e: e.activation(out=ex, in_=lf, func=AF.Exp), reads=[t_lf], writes=[t_ex])
        fw.op(fw.dve, lambda e: e.tensor_tensor(out=Kh, in0=kk, in1=ex, op=ALU.mult), reads=[t_kk, t_ex], writes=[t_Kh])
        return Kh, t_Kh

    def st_mm(h, d, Kh, t_Kh, v, t_v):
        kts = []
        for g in range(4):
            pb_, t_pb = c.psb.next()
            for slot in range(4):
                tk = (4 * g + slot) * 128
                fw.op(fw.pe, lambda e: e.transpose(out=pb_[:, slot, :], in_=Kh[:, tk:tk + 128], identity=c.ident), reads=[t_Kh, c.t_ident], writes=[t_pb])
            KT, t_KT = KTr.next()
            fw.op(fw.act, lambda e: e.activation(out=KT, in_=pb_[:, 0:4, :], func=AF.Copy), reads=[t_pb], writes=[t_KT])
            kts.append((KT, t_KT))
        ps, t_ps = c.ps.next()
        for g in range(4):
            KT, t_KT = kts[g]
            for slot in range(4):
                i = 4 * g + slot
                fw.op(fw.pe, lambda e: e.matmul(ps[:, 0:128], lhsT=KT[:, slot, :], rhs=v[:, i, :], start=(i == 0), stop=(i == NT - 1)),
                      reads=[t_KT, t_v], writes=[t_ps])
        so_ap, so_tok = S_par[d][0][:, h, :], S_par[d][1]
        fw.op(fw.dve, lambda e: e.tensor_scalar(out=so_ap, in0=ps[:, 0:128], scalar1=P.sel[:, d:d + 1], scalar2=None, op0=ALU.mult),
              reads=[t_ps, P.t_sel], writes=[so_tok])

    sg0, sg1, vv_ = proj_gate(0, 0), proj_gate(0, 1), proj_v(0)
    for h in range(8):
        k0 = st_elem(h, 0, *sg0)
        if h + 1 < 8:
            n0 = proj_gate(h + 1, 0)
        k1 = st_elem(h, 1, *sg1)
        if h + 1 < 8:
            n1 = proj_gate(h + 1, 1)
        st_mm(h, 0, *k0, *vv_)
        st_mm(h, 1, *k1, *vv_)
        if h + 1 < 8:
            vv_ = proj_v(h + 1)
            sg0, sg1 = n0, n1


def na_tiles(j):
    lo = min(j, 28)
    hi = max(j, 4) + 8
    return list(range(lo // 2, (hi - 1) // 2 + 1))


def na_slot_base():
    base = {}
    nxt = 9
    for j in range(32):
        if 4 <= j <= 28:
            base[j] = 0 if j % 2 == 0 else 4
        else:
            base[j] = nxt
            nxt += len(na_tiles(j))
    assert nxt == NSLOT
    return base


def emit_mixer(c, I):
    fw, A = c.fw, c.A
    m0 = A.mark()
    P = Ctx()
    gmix, t_gmix = load_vecT(c, I.norm_mix, 8)
    gmem, t_gmem = load_vecT(c, I.mem_norm, 8)
    P.gnorm, P.t_gnorm = A.alloc([1], F32)
    fw.dma(fw.sp, P.gnorm, I.gnorm.rearrange("(p o) -> p o", o=1), writes=[P.t_gnorm])
    P.sel, P.t_sel = A.alloc([2], F32)
    fw.dma(fw.sp, P.sel, I.sel, writes=[P.t_sel])
    P.epsc, P.t_epsc = A.alloc([1], F32)
    fw.op(fw.dve, lambda e: e.memset(P.epsc, EPS), writes=[P.t_epsc])
    lbsel, t_lbsel = A.alloc([DEPTH], F32)
    fw.dma(fw.sp, lbsel, I.lbsel, writes=[t_lbsel])
    lg, t_lg = A.alloc([DEPTH, 2, 8], F32)
    for l in range(DEPTH):
        for d in range(2):
            fw.dma(fw.sp, lg[:, l, d, :], I.lb_logits[l, d, :].rearrange("(h k) -> k h", k=128), writes=[t_lg], allow_slow_non_contiguous=True)
    fw.op(fw.act, lambda e: e.activation(out=lg, in_=lg, func=AF.Exp), reads=[t_lg], writes=[t_lg])
    P.lb, P.t_lb = A.alloc([2, 8], F32)
    P.oml, _ = A.alloc([2, 8], F32)
    P.noml, _ = A.alloc([2, 8], F32)
    tot, t_tot = A.alloc([2, 8], F32)
    fw.op(fw.dve, lambda e: e.tensor_tensor(out=tot, in0=lg[:, 0], in1=lg[:, 1], op=ALU.add), reads=[t_lg], writes=[t_tot])
    fw.op(fw.dve, lambda e: e.reciprocal(out=tot, in_=tot), reads=[t_tot], writes=[t_tot])
    fw.op(fw.dve, lambda e: e.tensor_scalar(out=P.lb, in0=lg[:, 0], scalar1=lbsel[:, 0:1], scalar2=None, op0=ALU.mult), reads=[t_lg, t_lbsel], writes=[P.t_lb])
    fw.op(fw.dve, lambda e: e.scalar_tensor_tensor(out=P.lb, in0=lg[:, 1], scalar=lbsel[:, 1:2], in1=P.lb, op0=ALU.mult, op1=ALU.add),
          reads=[t_lg, t_lbsel, P.t_lb], writes=[P.t_lb])
    fw.op(fw.dve, lambda e: e.tensor_tensor(out=P.lb, in0=P.lb, in1=tot, op=ALU.mult), reads=[P.t_lb, t_tot], writes=[P.t_lb])
    fw.op(fw.dve, lambda e: e.tensor_scalar(out=P.lb, in0=P.lb, scalar1=0.0, scalar2=1.0, op0=ALU.max, op1=ALU.min), reads=[P.t_lb], writes=[P.t_lb])
    fw.op(fw.dve, lambda e: e.tensor_scalar(out=P.oml, in0=P.lb, scalar1=-1.0, scalar2=1.0, op0=ALU.mult, op1=ALU.add), reads=[P.t_lb], writes=[P.t_lb])
    fw.op(fw.dve, lambda e: e.tensor_scalar(out=P.noml, in0=P.lb, scalar1=1.0, scalar2=-1.0, op0=ALU.mult, op1=ALU.add), reads=[P.t_lb], writes=[P.t_lb])
    S_par = [A.alloc([8, 128], BF16) for _ in range(2)]

    hT, t_h = I.HT[:, :, 0:T], I.t_HT
    if not I.h_ready:
        rmsnorm_T(c, I.x_load, NT, gmix, t_gmix, hT, t_h)
    m_ca = A.mark()
    build_ca(c, hT, t_h, I.mem, gmem, t_gmem, I.w_in, I.w_mem_kv, I.oca_s)
    A.release(m_ca)
    m_hp = A.mark()
    hpT, t_hp = A.alloc([8, T], BF16)
    rmsnorm_T(c, I.xp_load, NT, gmix, t_gmix, hpT, t_hp)

    m_na = A.mark()
    build_na(c, hT, t_h, hpT, t_hp, I.w_in, I.na_bias, I.ona_s)
    A.release(m_na)

    m2 = A.mark()
    hgrn_partner(c, P, hpT, t_hp, I.w_in, S_par)
    A.release(m2)
    A.release(m_hp)

    m5 = A.mark()
    hgrn_phase(c, P, hT, t_h, I.w_in, True, S_par, og_d=I.og_s)
    A.release(m5)

    build_tail(c, hT, t_h, I.x_load, I.x_store, I.w_in, I.og_s, I.ona_s, I.oca_s, I.w_hg_o, I.w_na_o, I.w_ca_o, I.w_out, I)
    A.release(m0)


def build_na(c, hT, t_h, hpT, t_hp, w_in_d, nab_d, ona_d):
    fw, A = c.fw, c.A
    wr = Ring([A.alloc([8, 128], BF16) for _ in range(3)])
    qbd, t_q = A.alloc([4, 32, 128], BF16)
    kT, t_k = A.alloc([4, 2560], BF16)
    vv, t_v = A.alloc([20, 512], BF16)
    onar = Ring([A.alloc([T], BF16) for _ in range(2)])
    fw.op(fw.dve, lambda e: e.memset(qbd, 0.0), writes=[t_q])

    def evq(ci, tt, ps, t_ps):
        for a in range(2):
            pa = slice(a * 64, (a + 1) * 64)
            fw.op(fw.act, lambda e: e.activation(out=qbd[pa, ci, tt * 8:(tt + 1) * 8, a * 64:(a + 1) * 64], in_=ps[pa, :].rearrange("p (j q) -> p j q", q=64),
                                                 func=AF.Copy, scale=0.125), reads=[t_ps], writes=[t_q])
    proj_fm(c, wr, [wsrc_cols(w_in_d, C_NQ + i * 128, 128) for i in range(4)], hT, t_h, 8, NTT, evq)

    def evk(ci, tt, ps, t_ps):
        fw.op(fw.act, lambda e: e.activation(out=kT[:, ci, 256 + tt * 512:256 + (tt + 1) * 512], in_=ps[:, :], func=AF.Copy), reads=[t_ps], writes=[t_k])

    def halo_k(ci, wb, t_wb):
        ps, t_ps = c.ps.next()
        for half, tk in enumerate((1792, 0)):
            for kc in range(8):
                fw.op(fw.pe, lambda e: e.matmul(ps[:, half * 256:(half + 1) * 256], lhsT=wb[:, kc, :], rhs=hpT[:, kc, tk:tk + 256],
                                                start=(kc == 0), stop=(kc == 7)), reads=[t_wb, t_hp], writes=[t_ps])
        fw.op(fw.act, lambda e: e.activation(out=kT[:, ci, 0:256], in_=ps[:, 0:256], func=AF.Copy), reads=[t_ps], writes=[t_k])
        fw.op(fw.act, lambda e: e.activation(out=kT[:, ci, 2304:2560], in_=ps[:, 256:512], func=AF.Copy), reads=[t_ps], writes=[t_k])
    proj_fm(c, wr, [wsrc_cols(w_in_d, C_NK + i * 128, 128) for i in range(4)], hT, t_h, 8, NTT, evk, extra=halo_k)
    wv, t_wv = A.alloc([8, 512], BF16)
    fw.dma(fw.pool, wv, wsrc_cols(w_in_d, C_NV, 512), writes=[t_wv])

    def evv(off):
        def f(g0, n, ps, t_ps):
            fw.op(fw.act, lambda e: e.activation(out=vv[:, off + g0, :], in_=ps[:, :], func=AF.Copy), reads=[t_ps], writes=[t_v])
        return f
    proj_tm(c, wv, t_wv, 512, hT, t_h, 8, [i * 128 for i in range(NT)], evv(2))
    proj_tm(c, wv, t_wv, 512, hpT, t_hp, 8, [1792, 1920], evv(0))
    proj_tm(c, wv, t_wv, 512, hpT, t_hp, 8, [0, 128], evv(18))

    sbase = na_slot_base()
    Etab = Ring([A.alloc([NSLOT, 128], BF16) for _ in range(2)])
    Per = Ring([A.alloc([6, 128], BF16) for _ in range(2)])
    Ptr = Ring([A.alloc([6, 128], BF16) for _ in range(2)])
    rc, t_rc = A.alloc([4, 64], F32)
    po, t_po = c.pacc[0]
    pd, t_pd = c.pacc[1]
    pd3 = pd[:, :].rearrange("p (j q) -> p j q", q=128)
    po3 = po[:, 0:256].rearrange("p (j q) -> p j q", q=64)
    for hp in range(4):
        onaT, t_ona = onar.next()
        E, t_E = Etab.next()
        fw.dma(fw.pool, E, nab_d[hp], writes=[t_E])
        fw.op(fw.act, lambda e: e.activation(out=E, in_=E, func=AF.Exp), reads=[t_E], writes=[t_E])

        def scores(j):
            tiles = na_tiles(j)
            banks = [c.ps.next() for _ in range((len(tiles) + 3) // 4)]
            for s, p in enumerate(tiles):
                ps, t_ps = banks[s // 4]
                fw.op(fw.pe, lambda e: e.matmul(ps[:, (s % 4) * 128:(s % 4 + 1) * 128], lhsT=kT[:, hp, p * 128:(p + 1) * 128], rhs=qbd[:, hp, j, :],
                                                start=True, stop=True), reads=[t_k, t_q], writes=[t_ps])
            Pe, t_Pe = Per.next()
            for bi, (ps, t_ps) in enumerate(banks):
                n = min(4, len(tiles) - 4 * bi)
                fw.op(fw.act, lambda e: e.activation(out=Pe[:, 4 * bi:4 * bi + n, :], in_=ps[:, :n * 128].rearrange("p (s q) -> p s q", q=128), func=AF.Exp),
                      reads=[t_ps], writes=[t_Pe])
            Pt, t_Pt = Ptr.next()
            nt = len(tiles)
            fw.op(fw.dve, lambda e: e.tensor_tensor(out=Pt[:, :nt, :], in0=Pe[:, :nt, :], in1=E[:, sbase[j]:sbase[j] + nt, :], op=ALU.mult),
                  reads=[t_Pe, t_E], writes=[t_Pt])
            return Pt, t_Pt

        def pv(j, Pt, t_Pt):
            tiles = na_tiles(j)
            nt = len(tiles)
            jj = j % 4
            for a in range(2):
                pa = slice(a * 64, (a + 1) * 64)
                hd = 2 * hp + a
                for s, p in enumerate(tiles):
                    fw.op(fw.pe, lambda e: e.matmul(po[pa, jj * 64:(jj + 1) * 64], lhsT=vv[:, p, hd * 64:(hd + 1) * 64], rhs=Pt[:, s, a * 64:(a + 1) * 64],
                                                    start=(s == 0), stop=(s == nt - 1)), reads=[t_v, t_Pt], writes=[t_po])
            for s, p in enumerate(tiles):
                fw.op(fw.pe, lambda e: e.matmul(pd[:, jj * 128:(jj + 1) * 128], lhsT=c.ones, rhs=Pt[:, s, :], start=(s == 0), stop=(s == nt - 1)),
                      reads=[c.t_ones, t_Pt], writes=[t_pd])
            if jj == 3:
                jb = j // 4
                for a in range(2):
                    pa = slice(a * 64, (a + 1) * 64)
                    fw.op(fw.dve, lambda e: e.reciprocal(out=rc[pa], in_=pd3[pa, :, a * 64:(a + 1) * 64]), reads=[t_pd], writes=[t_rc])
                    fw.op(fw.dve, lambda e: e.tensor_tensor(out=onaT[pa, jb * 256:(jb + 1) * 256].rearrange("p (j q) -> p j q", q=64), in0=po3[pa], in1=rc[pa], op=ALU.mult),
                          reads=[t_po, t_rc], writes=[t_ona])

        cur = scores(0)
        for j in range(32):
            nxt = scores(j + 1) if j + 1 < 32 else None
            pv(j, *cur)
            cur = nxt
        fw.dma(fw.sp, ona_d[hp * 128:(hp + 1) * 128, :], onaT, reads=[t_ona])


def build_ca(c, hT, t_h, mem_d, gmem, t_gmem, w_in_d, w_kv_d, oca_d):
    fw, A = c.fw, c.A
    memT, t_mem = A.alloc([8, 256], BF16)
    rmsnorm_T(c, dram_loader(c, mem_d), 2, gmem, t_gmem, memT, t_mem)
    wr = Ring([A.alloc([8, 128], BF16) for _ in range(3)])
    kcT, t_kc = A.alloc([4, 256], BF16)
    vca, t_vca = A.alloc([2, 512], BF16)
    ocaT, t_oca = A.alloc([4, T], BF16)
    for hh in range(4):
        wb, t_wb = wr.next()
        fw.dma(fw.pool, wb, wsrc_cols(w_kv_d, hh * 128, 128), writes=[t_wb])
        ps, t_ps = c.ps.next()
        for kc in range(8):
            fw.op(fw.pe, lambda e, kc=kc, wb=wb, ps=ps: e.matmul(ps[:, 0:256], lhsT=wb[:, kc, :], rhs=memT[:, kc, :], start=(kc == 0), stop=(kc == 7)),
                  reads=[t_wb, t_mem], writes=[t_ps])
        fw.op(fw.act, lambda e, ps=ps, hh=hh: e.activation(out=kcT[:, hh, :], in_=ps[:, 0:256], func=AF.Copy), reads=[t_ps], writes=[t_kc])
    wv, t_wv = A.alloc([8, 512], BF16)
    fw.dma(fw.pool, wv, wsrc_cols(w_kv_d, 512, 512), writes=[t_wv])

    def evv(g0, n, ps, t_ps):
        fw.op(fw.act, lambda e: e.activation(out=vca[:, g0, :], in_=ps[:, :], func=AF.Copy), reads=[t_ps], writes=[t_vca])
    proj_tm(c, wv, t_wv, 512, memT, t_mem, 8, [0, 128], evv)
    cq4, t_cq = A.alloc([NTT, 512], BF16)
    Pt8, t_Pt = A.alloc([NTT * 2, 512], BF16)
    rcr = Ring([A.alloc([512], F32) for _ in range(2)])
    for hh in range(4):
        wb, t_wb = wr.next()
        fw.dma(fw.pool, wb, wsrc_cols(w_in_d, C_CQ + hh * 128, 128), writes=[t_wb])
        for tt in range(NTT):
            ps, t_ps = c.ps.next()
            for kc in range(8):
                fw.op(fw.pe, lambda e: e.matmul(ps[:, :], lhsT=wb[:, kc, :], rhs=hT[:, kc, tt * 512:(tt + 1) * 512], start=(kc == 0), stop=(kc == 7)),
                      reads=[t_wb, t_h], writes=[t_ps])
            fw.op(fw.act, lambda e: e.activation(out=cq4[:, tt, :], in_=ps[:, :], func=AF.Copy, scale=128 ** -0.5), reads=[t_ps], writes=[t_cq])
        for tt in range(NTT):
            for mt in range(2):
                ps2, t_ps2 = c.ps.next()
                fw.op(fw.pe, lambda e: e.matmul(ps2[:, :], lhsT=kcT[:, hh, mt * 128:(mt + 1) * 128], rhs=cq4[:, tt, :], start=True, stop=True),
                      reads=[t_kc, t_cq], writes=[t_ps2])
                fw.op(fw.act, lambda e: e.activation(out=Pt8[:, tt * 2 + mt, :], in_=ps2[:, :], func=AF.Exp), reads=[t_ps2], writes=[t_Pt])
        for tt in range(NTT):
            po, t_po = c.ps.next()
            pd, t_pd = c.ps.next()
            for mt in range(2):
                fw.op(fw.pe, lambda e: e.matmul(po[:, :], lhsT=vca[:, mt, hh * 128:(hh + 1) * 128], rhs=Pt8[:, tt * 2 + mt, :], start=(mt == 0), stop=(mt == 1)),
                      reads=[t_vca, t_Pt], writes=[t_po])
            for mt in range(2):
                fw.op(fw.pe, lambda e: e.matmul(pd[:, :], lhsT=c.ones, rhs=Pt8[:, tt * 2 + mt, :], start=(mt == 0), stop=(mt == 1)),
                      reads=[c.t_ones, t_Pt], writes=[t_pd])
            rc, t_rc = rcr.next()
            fw.op(fw.dve, lambda e: e.reciprocal(out=rc, in_=pd[:, :]), reads=[t_pd], writes=[t_rc])
            fw.op(fw.dve, lambda e: e.tensor_tensor(out=ocaT[:, hh, tt * 512:(tt + 1) * 512], in0=po[:, :], in1=rc, op=ALU.mult),
                  reads=[t_po, t_rc], writes=[t_oca])
    fw.dma(fw.sp, oca_d.rearrange("(c p) t -> p c t", p=128), ocaT, reads=[t_oca])


def build_tail(c, hT, t_h, x_load, x_store, w_in_d, og_d, ona_d, oca_d, w_hg_o_d, w_na_o_d, w_ca_o_d, w_out_d, I):
    fw, A = c.fw, c.A
    fw.barrier()
    ogT, t_og = A.alloc([8, T], BF16)
    onaT, t_ona = A.alloc([4, T], BF16)
    ocaT, t_oca = A.alloc([4, T], BF16)
    fw.dma(fw.sp, ogT, og_d.rearrange("(c p) t -> p c t", p=128), writes=[t_og])
    fw.dma(fw.sp, onaT, ona_d.rearrange("(c p) t -> p c t", p=128), writes=[t_ona])
    fw.dma(fw.sp, ocaT, oca_d.rearrange("(c p) t -> p c t", p=128), writes=[t_oca])
    mT, t_m = A.alloc([8, T], BF16)
    wo = Ring([A.alloc([8, 128], BF16) for _ in range(4)])
    wg = Ring([A.alloc([8, 128], BF16) for _ in range(4)])
    sgr = Ring([A.alloc([512], F32) for _ in range(2)])
    accr = Ring([A.alloc([512], F32) for _ in range(2)])
    tmpr = Ring([A.alloc([512], F32) for _ in range(2)])
    branches = [(ogT, t_og, 8, w_hg_o_d, C_GHG), (onaT, t_ona, 4, w_na_o_d, C_GNA), (ocaT, t_oca, 4, w_ca_o_d, C_GCA)]
    for m in range(8):
        ws = []
        for (oT, t_o, KC, wod, gcol) in branches:
            wb, t_wb = wo.next()
            fw.dma(fw.pool, wb[:, :KC, :], wod[:, m * 128:(m + 1) * 128].rearrange("(kc p) n -> p kc n", p=128), writes=[t_wb])
            wgb, t_wgb = wg.next()
            fw.dma(fw.pool, wgb, wsrc_cols(w_in_d, gcol + m * 128, 128), writes=[t_wgb])
            ws.append((wb, t_wb, wgb, t_wgb))
        for tt in range(NTT):
            sl = slice(tt * 512, (tt + 1) * 512)
            acc, t_acc = accr.next()
            for bi, (oT, t_o, KC, wod, gcol) in enumerate(branches):
                wb, t_wb, wgb, t_wgb = ws[bi]
                psy, t_psy = c.ps.next()
                for kc in range(KC):
                    fw.op(fw.pe, lambda e, kc=kc, wb=wb, psy=psy, oT=oT, KC=KC: e.matmul(psy[:, :], lhsT=wb[:, kc, :], rhs=oT[:, kc, sl], start=(kc == 0), stop=(kc == KC - 1)),
                          reads=[t_wb, t_o], writes=[t_psy])
                psg, t_psg = c.ps.next()
                for kc in range(8):
                    fw.op(fw.pe, lambda e, kc=kc, wgb=wgb, psg=psg: e.matmul(psg[:, :], lhsT=wgb[:, kc, :], rhs=hT[:, kc, sl], start=(kc == 0), stop=(kc == 7)),
                          reads=[t_wgb, t_h], writes=[t_psg])
                sg, t_sg = sgr.next()
                fw.op(fw.act, lambda e, psg=psg, sg=sg: e.activation(out=sg, in_=psg[:, :], func=AF.Sigmoid), reads=[t_psg], writes=[t_sg])
                if bi == 0:
                    fw.op(fw.dve, lambda e, psy=psy, sg=sg, acc=acc: e.tensor_tensor(out=acc, in0=psy[:, :], in1=sg, op=ALU.mult), reads=[t_psy, t_sg], writes=[t_acc])
                else:
                    tmp, t_tmp = tmpr.next()
                    fw.op(fw.dve, lambda e, psy=psy, sg=sg, tmp=tmp: e.tensor_tensor(out=tmp, in0=psy[:, :], in1=sg, op=ALU.mult), reads=[t_psy, t_sg], writes=[t_tmp])
                    if bi == 1:
                        fw.op(fw.dve, lambda e, tmp=tmp, acc=acc: e.tensor_tensor(out=acc, in0=acc, in1=tmp, op=ALU.add), reads=[t_tmp, t_acc], writes=[t_acc])
                    else:
                        fw.op(fw.dve, lambda e, tmp=tmp, acc=acc, m=m: e.tensor_tensor(out=mT[:, m, sl], in0=acc, in1=tmp, op=ALU.add), reads=[t_tmp, t_acc], writes=[t_m])
    gffn, t_gffn = load_vecT(c, I.norm_ffn, 8)
    wout, t_wout = A.alloc([8, 1024], BF16)
    for half in range(2):
        fw.dma(fw.pool, wout[:, :, half * 512:(half + 1) * 512], wsrc_cols(w_out_d, half * 512, 512), writes=[t_wout])
    xr = Ring([A.alloc([1024], F32) for _ in range(3)])
    pend = None
    for i in range(NT):
        xt, t_x = xr.next()
        x_load(i, xt, t_x)
        for half in range(2):
            ps, t_ps = c.ps.next()
            for kc in range(8):
                fw.op(fw.pe, lambda e, kc=kc, ps=ps, half=half, i=i: e.matmul(ps[:, :], lhsT=mT[:, kc, i * 128:(i + 1) * 128], rhs=wout[:, kc, half * 512:(half + 1) * 512],
                                                                             start=(kc == 0), stop=(kc == 7)), reads=[t_m, t_wout], writes=[t_ps])
            fw.op(fw.dve, lambda e, ps=ps, xt=xt, half=half: e.tensor_tensor(out=xt[:, half * 512:(half + 1) * 512], in0=xt[:, half * 512:(half + 1) * 512], in1=ps[:, :], op=ALU.add),
                  reads=[t_ps, t_x], writes=[t_x])
        x_store(i, xt, t_x)
        if pend is not None:
            norm_tile_b(c, i - 1, *pend, gffn, t_gffn, hT, t_h)
        pend = norm_tile_a(c, I.NS, xt, t_x)
    norm_tile_b(c, NT - 1, *pend, gffn, t_gffn, hT, t_h)


def emit_ffn(c, I):
    if True:
        fw, A = c.fw, c.A
        m0 = A.mark()
        norm_d, w_up_d, conv_w_d, conv_b_d, w_down_d, nf_d = I.norm_ffn, I.w_up, I.conv_w, I.conv_b, I.w_down, I.norm_final
        gffn, t_gffn = load_vecT(c, norm_d, 8)
        cw, t_cw = A.alloc([3, 44], F32)
        for tap in range(3):
            fw.dma(fw.sp, cw[:, tap, :], conv_w_d[tap, :].rearrange("(c p) -> p c", p=128), writes=[t_cw], allow_slow_non_contiguous=True)
        cb, t_cb = load_vecT(c, conv_b_d, 44)
        actT, t_act = A.alloc([NFF, T], BF16)
        m1 = A.mark()
        h2T, t_h2 = I.HT, I.t_HT
        rmsnorm_T(c, I.halo_load, 1, gffn, t_gffn, h2T, t_h2, col0=T, rows=2)
        wr = Ring([A.alloc([8, 128], BF16) for _ in range(4)])
        ur = Ring([A.alloc([T + 2], F32) for _ in range(2)])
        ucr = Ring([A.alloc([T], F32) for _ in range(2)])
        x2r = Ring([A.alloc([T], F32) for _ in range(2)])
        for j in range(NFF):
            x2, t_x2 = x2r.next()
            ucs = []
            for half in range(2):
                ci = j + half * NFF
                wb, t_wb = wr.next()
                fw.dma(fw.pool, wb, wsrc_cols(w_up_d, ci * 128, 128), writes=[t_wb])
                u, t_u = ur.next()
                for tt in range(NTT):
                    ps, t_ps = c.ps.next()
                    for kc in range(8):
                        fw.op(fw.pe, lambda e, kc=kc, wb=wb, ps=ps, tt=tt: e.matmul(ps[:, :], lhsT=wb[:, kc, :], rhs=h2T[:, kc, tt * 512:(tt + 1) * 512],
                                                                                   start=(kc == 0), stop=(kc == 7)), reads=[t_wb, t_h2], writes=[t_ps])
                    fw.op(fw.act, lambda e, ps=ps, u=u, tt=tt: e.activation(out=u[:, 1 + tt * 512:1 + (tt + 1) * 512], in_=ps[:, :], func=AF.Copy), reads=[t_ps], writes=[t_u])
                ps, t_ps = c.ps.next()
                for kc in range(8):
                    fw.op(fw.pe, lambda e, kc=kc, wb=wb, ps=ps: e.matmul(ps[:, 0:2], lhsT=wb[:, kc, :], rhs=h2T[:, kc, T:T + 2], start=(kc == 0), stop=(kc == 7)),
                          reads=[t_wb, t_h2], writes=[t_ps])
                fw.op(fw.act, lambda e, ps=ps, u=u: e.activation(out=u[:, 0:1], in_=ps[:, 0:1], func=AF.Copy), reads=[t_ps], writes=[t_u])
                fw.op(fw.act, lambda e, ps=ps, u=u: e.activation(out=u[:, T + 1:T + 2], in_=ps[:, 1:2], func=AF.Copy), reads=[t_ps], writes=[t_u])
                uc, t_uc = ucr.next()
                fw.op(fw.act, lambda e, u=u, uc=uc, ci=ci: e.activation(out=uc, in_=u[:, 1:T + 1], func=AF.Identity, scale=cw[:, 1, ci:ci + 1], bias=cb[:, ci:ci + 1]),
                      reads=[t_u, t_cw, t_cb], writes=[t_uc])
                fw.op(fw.dve, lambda e, u=u, uc=uc, ci=ci: e.scalar_tensor_tensor(out=uc, in0=u[:, 0:T], scalar=cw[:, 0, ci:ci + 1], in1=uc, op0=ALU.mult, op1=ALU.add),
                      reads=[t_u, t_cw, t_uc], writes=[t_uc])
                fw.op(fw.dve, lambda e, u=u, uc=uc, ci=ci: e.scalar_tensor_tensor(out=uc, in0=u[:, 2:T + 2], scalar=cw[:, 2, ci:ci + 1], in1=uc, op0=ALU.mult, op1=ALU.add),
                      reads=[t_u, t_cw, t_uc], writes=[t_uc])
                ucs.append((uc, t_uc))
            (ua, t_ua), (ug, t_ug) = ucs
            fw.op(fw.act, lambda e: e.activation(out=x2, in_=ua, func=AF.Gelu_apprx_tanh), reads=[t_ua], writes=[t_x2])
            fw.op(fw.dve, lambda e: e.tensor_tensor(out=actT[:, j, :], in0=x2, in1=ug, op=ALU.mult), reads=[t_x2, t_ug], writes=[t_act])
        A.release(m1)
        wd, t_wd = A.alloc([NFF, 1024], BF16)
        for j0 in range(0, NFF, 2):
            fw.dma(fw.pool, wd[:, j0:j0 + 2, :], w_down_d[j0 * 128:(j0 + 2) * 128, :].rearrange("(j p) n -> p j n", p=128), writes=[t_wd])
        if I.x_store is not None:
            gnext, t_gnext = load_vecT(c, I.norm_next, 8)
        gfin, t_gfin = A.alloc([1024], F32)
        fw.dma(fw.sp, gfin, nf_d.partition_broadcast(128), writes=[t_gfin])
        xr = Ring([A.alloc([1024], F32) for _ in range(3)])
        yr = Ring([A.alloc([1024], F32) for _ in range(2)])
        junk, t_junk = A.alloc([1024], BF16)
        ssr = Ring([A.alloc([1], F32) for _ in range(2)])
        pend = None
        for i in range(NT):
            xt, t_x = xr.next()
            I.x_load(i, xt, t_x)
            for half in range(2):
                ps, t_ps = c.ps.next()
                for j in range(NFF):
                    fw.op(fw.pe, lambda e, j=j, ps=ps, half=half, i=i: e.matmul(ps[:, :], lhsT=actT[:, j, i * 128:(i + 1) * 128], rhs=wd[:, j, half * 512:(half + 1) * 512],
                                                                               start=(j == 0), stop=(j == NFF - 1)), reads=[t_act, t_wd], writes=[t_ps])
                fw.op(fw.dve, lambda e, ps=ps, xt=xt, half=half: e.tensor_tensor(out=xt[:, half * 512:(half + 1) * 512], in0=xt[:, half * 512:(half + 1) * 512], in1=ps[:, :], op=ALU.add),
                      reads=[t_ps, t_x], writes=[t_x])
            if I.x_store is not None:
                I.x_store(i, xt, t_x)
                if pend is not None:
                    norm_tile_b(c, i - 1, *pend, gnext, t_gnext, I.HT, I.t_HT)
                pend = norm_tile_a(c, I.NS, xt, t_x)
            if I.y_store is None:
                continue
            ss, t_ss = ssr.next()
            fw.op(fw.act, lambda e, xt=xt, ss=ss: e.activation(out=junk, in_=xt, func=AF.Square, accum_out=ss), reads=[t_x], writes=[t_junk, t_ss])
            fw.op(fw.act, lambda e, ss=ss: e.activation(out=ss, in_=ss, func=AF.Sqrt, scale=1.0 / 1024, bias=EPS), reads=[t_ss], writes=[t_ss])
            fw.op(fw.dve, lambda e, ss=ss: e.reciprocal(out=ss, in_=ss), reads=[t_ss], writes=[t_ss])
            yt, t_y = yr.next()
            fw.op(fw.dve, lambda e, xt=xt, yt=yt, ss=ss: e.scalar_tensor_tensor(out=yt, in0=xt, scalar=ss[:, 0:1], in1=gfin, op0=ALU.mult, op1=ALU.mult),
                  reads=[t_x, t_ss, t_gfin], writes=[t_y])
            I.y_store(i, yt, t_y)
        if pend is not None:
            norm_tile_b(c, NT - 1, *pend, gnext, t_gnext, I.HT, I.t_HT)
        A.release(m0)


def na_bias_table(rpb, half):
    base = 32 * half
    qc = np.arange(64)
    col_start = np.clip(qc - 8, 0, 48)
    kp = np.arange(128)
    krow_off = kp // 64
    kcol = kp % 64
    colvalid = (kcol[:, None] >= col_start[None, :]) & (kcol[:, None] < col_start[None, :] + 16)
    dc = np.clip(kcol[:, None] - qc[None, :], -15, 15) + 15
    out = np.full((8, 128, NSLOT, 64), NEG, np.float32)
    sbase = na_slot_base()
    done = set()
    for j in range(32):
        b0 = sbase[j]
        if b0 in done:
            continue
        done.add(b0)
        r = base + j
        start = int(np.clip(r - 4, 0, 56))
        for s, p in enumerate(na_tiles(j)):
            R = base - 4 + 2 * p + krow_off
            rowvalid = (R >= start) & (R < start + 8)
            dr = np.clip(R - r + 7, 0, 14)
            valid = rowvalid[:, None] & colvalid
            vals = rpb[:, dr[:, None], dc]
            out[:, :, b0 + s, :] = np.where(valid[None], vals, np.float32(NEG))
    return np.ascontiguousarray(out.reshape(4, 2, 128, NSLOT, 64).transpose(0, 2, 3, 1, 4).reshape(4, 128, NSLOT, 128))


PAIRS = [[0, 1], [2, 3], [4, 5], [6, 7]]


def build_fused():
    nc = bass.Bass("TRN2", target_bir_lowering=False)
    dt = lambda name, shape, dtype=F32, kind="ExternalInput": nc.dram_tensor(name, shape, dtype, kind=kind).ap()
    x_d = dt("x", [T, D])
    xp_d = dt("xp", [T, D])
    mem_d = dt("mem", [256, D])
    norm_mix_d = dt("norm_mix", [DEPTH, D])
    w_in_d = dt("w_in", [DEPTH, D, IN_COLS])
    lbl_d = dt("lb_logits", [DEPTH, 2, D])
    lbsel_d = dt("lbsel", [128, DEPTH, DEPTH])
    gnorm_d = dt("gnorm", [DEPTH, 128])
    w_hg_o_d = dt("w_hg_o", [DEPTH, D, D])
    nab_d = dt("na_bias", [DEPTH, 4, 128, NSLOT, 128])
    w_na_o_d = dt("w_na_o", [DEPTH, 512, D])
    mem_norm_d = dt("mem_norm", [D])
    w_kv_d = dt("w_mem_kv", [DEPTH, D, D])
    w_ca_o_d = dt("w_ca_o", [DEPTH, 512, D])
    w_out_d = dt("w_out", [DEPTH, D, D])
    sel_d = dt("sel", [128, 2])
    selh_d = dt("selh", [2, 1])
    norm_ffn_d = dt("norm_ffn", [DEPTH, D])
    w_up_d = dt("w_up", [DEPTH, D, 2 * D_FF])
    conv_w_d = dt("conv_w", [DEPTH, 3, 2 * D_FF])
    conv_b_d = dt("conv_b", [DEPTH, 2 * D_FF])
    w_down_d = dt("w_down", [DEPTH, D_FF, D])
    nf_d = dt("norm_final", [D])
    ident_d = dt("ident", [128, 128])
    tri_d = dt("tri", [128, 2, 64])
    y_d = dt("y", [T, D], kind="ExternalOutput")
    it = lambda name, shape, dtype=F32: dt(name, shape, dtype, kind="Internal")
    og_d, ona_d, oca_d = it("og_s", [D, T], BF16), it("ona_s", [512, T], BF16), it("oca_s", [512, T], BF16)
    xmid_d = [it(f"xmid{l}", [T, D]) for l in range(DEPTH)]
    CH = 512
    NCH = T // CH
    xnext_d = [[it(f"xnext{l}_{k}", [CH, D]) for k in range(NCH)] for l in range(DEPTH - 1)]
    xg_d = [[it(f"xg{l}_{k}", [2 * CH, D]) for k in range(NCH)] for l in range(DEPTH - 1)]
    hbi_d = [it(f"hbi{l}", [2, D]) for l in range(DEPTH)]
    hbo_d = [it(f"hbo{l}", [4, D]) for l in range(DEPTH)]

    with ExitStack() as ctx:
        c = setup(nc, ctx, 53000)
        fw, A = c.fw, c.A
        block = ctx.enter_context(nc.Block())
        load_consts(c, ident_d, tri_d)
        sel, t_sel = A.alloc([2], F32)
        fw.dma(fw.sp, sel, sel_d[:, :], writes=[t_sel])
        selh, t_selh = A.alloc([1], F32)
        fw.dma(fw.sp, selh[:2], selh_d[:, :], writes=[t_selh])

        HT, t_HT = A.alloc([8, T + 2], BF16)
        NS = norm_scratch(c)

        def store_to(dst):
            def f(i, xt, t_x):
                fw.dma(fw.sp, dst[i * 128:(i + 1) * 128, :], xt, reads=[t_x])
            return f

        def store_chunks(dsts):
            def f(i, xt, t_x):
                r = (i * 128) % CH
                fw.dma(fw.sp, dsts[(i * 128) // CH][r:r + 128, :], xt, reads=[t_x])
            return f

        def load_chunks(srcs):
            def f(i, xt, t_x):
                r = (i * 128) % CH
                fw.dma(fw.sp, xt, srcs[(i * 128) // CH][r:r + 128, :], writes=[t_x])
            return f

        xg_deps = None
        for l in range(DEPTH):
            mk = A.mark()
            I = Ctx()
            if l == 0:
                I.x_load = dram_loader(c, x_d)
                I.xp_load = dram_loader(c, xp_d)
            else:
                I.x_load = load_chunks(xnext_d[l - 1])
                xb, t_xb = A.alloc([1024], F32)
                xg = xg_d[l - 1]

                def xp_load(i, xt, t_x, xg=xg, xb=xb, t_xb=t_xb):
                    r = (i * 128) % CH
                    g = xg[(i * 128) // CH]
                    fw.sp.wait(xg_deps[(i * 128) // CH])
                    fw.dma(fw.sp, xt, g[r:r + 128, :], writes=[t_x])
                    fw.dma(fw.sp, xb, g[CH + r:CH + r + 128, :], writes=[t_xb])
                    fw.op(fw.dve, lambda e: e.tensor_scalar(out=xt, in0=xt, scalar1=sel[:, 0:1], scalar2=None, op0=ALU.mult), reads=[t_x, t_sel], writes=[t_x])
                    fw.op(fw.dve, lambda e: e.scalar_tensor_tensor(out=xt, in0=xb, scalar=sel[:, 1:2], in1=xt, op0=ALU.mult, op1=ALU.add),
                          reads=[t_xb, t_sel, t_x], writes=[t_x])
                I.xp_load = xp_load
            I.HT, I.t_HT, I.NS, I.h_ready, I.norm_ffn = HT, t_HT, NS, (l > 0), norm_ffn_d[l]
            I.x_store = store_to(xmid_d[l])
            I.mem, I.norm_mix, I.w_in, I.lb_logits, I.lbsel = mem_d, norm_mix_d[l], w_in_d[l], lbl_d, lbsel_d[:, l, :]
            I.gnorm, I.w_hg_o, I.na_bias, I.w_na_o, I.mem_norm = gnorm_d[l], w_hg_o_d[l], nab_d[l], w_na_o_d[l], mem_norm_d
            I.w_mem_kv, I.w_ca_o, I.w_out, I.sel = w_kv_d[l], w_ca_o_d[l], w_out_d[l], sel_d[:, :]
            I.og_s, I.ona_s, I.oca_s = og_d, ona_d, oca_d
            emit_mixer(c, I)
            A.release(mk)
            fw.dma(fw.sp, hbi_d[l][0:1, :], xmid_d[l][0:1, :])
            fw.dma(fw.sp, hbi_d[l][1:2, :], xmid_d[l][T - 1:T, :])
            hb_dep = fw.allgather(hbi_d[l][:, :], hbo_d[l][:, :], PAIRS)
            J = Ctx()
            J.x_load = dram_loader(c, xmid_d[l])

            def halo_load(i, xt, t_x, hbo=hbo_d[l], hb_dep=hb_dep):
                fw.sp.wait(hb_dep)
                fw.dma(fw.sp, xt[:2], hbo[1:3, :], writes=[t_x])
                fw.op(fw.dve, lambda e: e.tensor_scalar(out=xt[:2], in0=xt[:2], scalar1=selh[:2, 0:1], scalar2=None, op0=ALU.mult), reads=[t_x, t_selh], writes=[t_x])
            J.halo_load = halo_load
            last = (l == DEPTH - 1)
            J.HT, J.t_HT, J.NS = HT, t_HT, NS
            J.norm_next = None if last else norm_mix_d[l + 1]
            J.x_store = None if last else store_chunks(xnext_d[l])
            J.y_store = store_to(y_d) if last else None
            J.norm_ffn, J.w_up, J.conv_w, J.conv_b, J.w_down, J.norm_final = norm_ffn_d[l], w_up_d[l], conv_w_d[l], conv_b_d[l], w_down_d[l], nf_d
            emit_ffn(c, J)
            if not last:
                xg_deps = [fw.allgather(xnext_d[l][k][:, :], xg_d[l][k][:, :], PAIRS) for k in range(NCH)]
        fw.finish()
        fw.emit(block)
    return nc


_PROG = []


def kernel(x, mem, norm_mix, w_in, hg_lb_logits, hg_gnorm, w_hg_o, na_rpb, w_na_o, mem_norm,
           w_mem_kv, w_ca_o, w_out, norm_ffn, w_up, conv_w, conv_b, w_down, norm_final):
    f = lambda a: np.ascontiguousarray(np.asarray(a, dtype=np.float32))
    x = f(x)
    B = x.shape[0]
    ident = np.eye(128, dtype=np.float32)
    s_idx = np.arange(128) % 64
    t_idx = np.arange(64)
    tri = np.stack([(s_idx[:, None] <= t_idx[None, :]), (s_idx[:, None] >= t_idx[None, :])], axis=1).astype(np.float32)
    cores = [(b, hf) for b in range(B) for hf in range(2)]
    lbsel = np.zeros((128, DEPTH, DEPTH), np.float32)
    for l in range(DEPTH):
        lbsel[:, l, 1:l + 1] = 1.0
    shared = {
        "norm_mix": f(norm_mix), "w_in": f(w_in), "lb_logits": f(hg_lb_logits), "lbsel": lbsel, "gnorm": f(hg_gnorm),
        "w_hg_o": f(w_hg_o), "w_na_o": f(w_na_o), "mem_norm": f(mem_norm), "w_mem_kv": f(w_mem_kv), "w_ca_o": f(w_ca_o),
        "w_out": f(w_out), "norm_ffn": f(norm_ffn), "w_up": f(w_up), "conv_w": f(conv_w), "conv_b": f(conv_b),
        "w_down": f(w_down), "norm_final": f(norm_final), "ident": ident, "tri": tri,
    }
    nab = [np.stack([na_bias_table(f(na_rpb[l]), hf) for l in range(DEPTH)]) for hf in range(2)]
    in_maps = []
    for (b, hf) in cores:
        sel = np.zeros((128, 2), np.float32)
        sel[:, 0] = 1.0 if hf == 1 else 0.0
        sel[:, 1] = 1.0 if hf == 0 else 0.0
        selh = np.array([[1.0 if hf == 1 else 0.0], [1.0 if hf == 0 else 0.0]], np.float32)
        m = dict(shared)
        m.update({"x": f(x[b, hf * T:(hf + 1) * T]), "xp": f(x[b, (1 - hf) * T:(2 - hf) * T]), "mem": f(mem[b]),
                  "na_bias": nab[hf], "sel": sel, "selh": selh})
        in_maps.append(m)
    if not _PROG:
        _PROG.append(build_fused())
    res = run_bass_kernel_spmd(_PROG[0], in_maps, core_ids=list(range(8)))
    y = np.empty_like(x)
    for ci, (b, hf) in enumerate(cores):
        y[b, hf * T:(hf + 1) * T] = res.results[ci]["y"]
    return y
```

```python
import numpy as np
from contextlib import ExitStack

import concourse.bass as bass
import concourse.mybir as mybir
from concourse.bass_utils import run_bass_kernel_spmd

F32 = mybir.dt.float32
BF16 = mybir.dt.bfloat16
AF = mybir.ActivationFunctionType
ALU = mybir.AluOpType

D = 1024
T = 2048
NT = T // 128
NTT = T // 512
DEPTH = 2
IN_COLS = 10240
D_FF = 2816
NFF = D_FF // 128
EPS = 1e-6
LOGF_FLOOR = float(np.log(1e-12))
NEG = -30000.0
NSLOT = 47
EPOCH = 12000
DEBUG = False

C_HQ, C_HI, C_HFF, C_HFB, C_HG = 0, 1024, 2048, 3072, 4096
C_NQ, C_NK, C_NV, C_CQ = 5120, 5632, 6144, 6656
C_GHG, C_GNA, C_GCA = 7168, 8192, 9216


class _Rec:
    def __init__(self):
        self.call = None

    def __getattr__(self, name):
        def f(*a, **k):
            self.call = (name, a, k)
            return self
        return f


class Tok:
    __slots__ = ("w", "r", "rd")

    def __init__(self):
        self.w = None
        self.r = {}
        self.rd = []


class Eng:
    def __init__(self, fw, idx, name, nsem):
        self.fw, self.idx, self.name = fw, idx, name
        self.n = 0
        self.seen = {}
        self.prog = []
        self.sems = [fw.ctx.enter_context(fw.nc.semaphore(f"s_{name}_{i}")) for i in range(nsem)]

    def wait(self, d):
        if d is None:
            return
        key = d[:2]
        val = d[2]
        if self.seen.get(key, 0) >= val:
            return
        self.seen[key] = val
        if d[0] == "E":
            e = self.fw.engs[d[1]]
            ep = (val - 1) // EPOCH
            sem, v = e.sems[ep], val - ep * EPOCH
        else:
            sem, v = self.fw.dsems[d[1]], val
        self.prog.append(lambda h, sem=sem, v=v: h.wait_ge(sem, v))


class FW:
    def __init__(self, nc, ctx, ndsem=32, nsem=10):
        self.nc, self.ctx = nc, ctx
        self.engs = []
        for name, ns in (("pe", nsem), ("act", nsem), ("dve", nsem), ("pool", nsem), ("sp", 1)):
            self.engs.append(Eng(self, len(self.engs), name, ns))
        self.pe, self.act, self.dve, self.pool, self.sp = self.engs
        self.dsems = [ctx.enter_context(nc.semaphore(f"s_dma_{i}")) for i in range(ndsem)]
        self.dval = [0] * ndsem
        half = ndsem // 2
        self.qsems = {self.sp.idx: list(range(0, half)), self.pool.idx: list(range(half, ndsem))}
        self.qnext = {self.sp.idx: 0, self.pool.idx: 0}

    def op(self, eng, fn, reads=(), writes=()):
        pe = eng is self.pe
        for t in reads:
            if t.w is not None and not (pe and t.w[0] == "E" and t.w[1] == eng.idx):
                eng.wait(t.w)
        for t in writes:
            if t.w is not None and not (pe and t.w[0] == "E" and t.w[1] == eng.idx):
                eng.wait(t.w)
            for ei, n in t.r.items():
                if pe and ei == eng.idx:
                    continue
                eng.wait(("E", ei, n))
            for d in t.rd:
                eng.wait(d)
        eng.n += 1
        assert eng.n <= EPOCH * len(eng.sems), f"too many instructions on {eng.name}"
        ep = (eng.n - 1) // EPOCH
        rec = _Rec()
        fn(rec)
        name, a, k = rec.call
        eng.prog.append(lambda h, name=name, a=a, k=k, sem=eng.sems[ep]: getattr(h, name)(*a, **k).then_inc(sem, 1))
        for t in reads:
            if t.r.get(eng.idx, 0) < eng.n:
                t.r[eng.idx] = eng.n
        me = ("E", eng.idx, eng.n)
        for t in writes:
            t.w = me
            t.r = {}
            t.rd = []

    def dma(self, q, out, in_, reads=(), writes=(), **kw):
        for t in reads:
            q.wait(t.w)
        for t in writes:
            q.wait(t.w)
            for ei, n in t.r.items():
                q.wait(("E", ei, n))
            for d in t.rd:
                q.wait(d)
        lst = self.qsems[q.idx]
        si = lst[self.qnext[q.idx]]
        self.qnext[q.idx] = (self.qnext[q.idx] + 1) % len(lst)
        if self.dval[si] > 0:
            q.wait(("D", si, self.dval[si]))
        self.dval[si] += 16
        q.prog.append(lambda h, out=out, in_=in_, kw=kw, sem=self.dsems[si]:
                      h.dma_start(out=out, in_=in_, **kw).then_inc(sem, 16))
        me = ("D", si, self.dval[si])
        for t in reads:
            t.rd.append(me)
            if len(t.rd) > 64:
                t.rd = t.rd[-64:]
        for t in writes:
            t.w = me
            t.r = {}
            t.rd = []

    def allgather(self, cin, cout, groups):
        self.barrier()
        sem = self.ctx.enter_context(self.nc.semaphore(f"s_cc_{len(self.dsems)}"))
        self.dsems.append(sem)
        self.dval.append(1)
        self.pool.prog.append(lambda h, sem=sem: h.collective_compute("AllGather", ALU.bypass, ins=[cin], outs=[cout],
                                                                     replica_groups=groups).then_inc(sem))
        return ("D", len(self.dsems) - 1, 1)

    def barrier(self):
        deps = [("E", e.idx, e.n) for e in self.engs[:4] if e.n > 0]
        deps += [("D", i, v) for i, v in enumerate(self.dval) if v > 0]
        for e in self.engs:
            for d in deps:
                if not (d[0] == "E" and d[1] == e.idx):
                    e.wait(d)

    def finish(self):
        for i, v in enumerate(self.dval):
            if v > 0:
                self.sp.wait(("D", i, v))

    def emit(self, block):
        def run(eng):
            def f(h):
                for p in eng.prog:
                    p(h)
            return f
        block.tensor(run(self.pe))
        block.scalar(run(self.act))
        block.vector(run(self.dve))
        block.gpsimd(run(self.pool))
        block.sync(run(self.sp))


class Arena:
    def __init__(self, nc, ctx, fw, words):
        self.t = ctx.enter_context(nc.sbuf_tensor("arena", [128, words], F32))
        self.words, self.off, self.fw = words, 0, fw

    def alloc(self, shape, dt):
        n = int(np.prod(shape))
        w = n if dt == F32 else (n + 1) // 2
        w = (w + 7) // 8 * 8
        assert self.off + w <= self.words, f"SBUF arena overflow: {self.off}+{w}>{self.words}"
        ap = self.t[:, self.off:self.off + w]
        self.off += w
        if dt != F32:
            ap = ap.bitcast(dt)
            ap = ap[:, :n]
        else:
            ap = ap[:, :n]
        if len(shape) == 2:
            ap = ap.rearrange("p (a b) -> p a b", a=shape[0])
        elif len(shape) == 3:
            ap = ap.rearrange("p (a b c) -> p a b c", a=shape[0], b=shape[1])
        return ap, Tok()

    def mark(self):
        return self.off

    def release(self, m):
        self.fw.barrier()
        self.off = m


class Ring:
    def __init__(self, items):
        self.items, self.i = items, 0

    def next(self):
        it = self.items[self.i]
        self.i = (self.i + 1) % len(self.items)
        return it


class Ctx:
    pass


def setup(nc, ctx, arena_words):
    c = Ctx()
    c.nc = nc
    c.fw = FW(nc, ctx)
    c.A = Arena(nc, ctx, c.fw, arena_words)
    banks = []
    for i in range(6):
        t = ctx.enter_context(nc.psum_tensor(f"psf{i}", [128, 512], F32))
        banks.append((t, Tok()))
    c.ps = Ring(banks[:4])
    c.pacc = banks[4:]
    bb = []
    for i in range(2):
        t = ctx.enter_context(nc.psum_tensor(f"psb{i}", [128, 8, 128], BF16))
        bb.append((t, Tok()))
    c.psb = Ring(bb)
    return c


def load_consts(c, ident_d, tri_d):
    fw, A = c.fw, c.A
    c.ident, c.t_ident = A.alloc([128], BF16)
    fw.dma(fw.pool, c.ident, ident_d[:, :], writes=[c.t_ident])
    c.tri, c.t_tri = A.alloc([2, 64], BF16)
    fw.dma(fw.pool, c.tri, tri_d[:, :, :], writes=[c.t_tri])
    c.ones, c.t_ones = A.alloc([128], BF16)
    fw.op(fw.dve, lambda e: e.memset(c.ones, 1.0), writes=[c.t_ones])
    c.onesd, c.t_onesd = A.alloc([128], BF16)
    fw.op(fw.dve, lambda e: e.memset(c.onesd, 1.0 / 128), writes=[c.t_onesd])


def load_vecT(c, vec_d, n):
    ap, tok = c.A.alloc([n], F32)
    c.fw.dma(c.fw.sp, ap, vec_d.rearrange("(c p) -> p c", p=128), writes=[tok], allow_slow_non_contiguous=True)
    return ap, tok


def dram_loader(c, x_d, rows=128):
    def load(i, xt, t_x):
        c.fw.dma(c.fw.sp, xt[:rows], x_d[i * rows:(i + 1) * rows, :], writes=[t_x])
    return load


def rmsnorm_T(c, load, ntiles, gT, t_g, hT, t_h, col0=0, rows=128):
    fw, A = c.fw, c.A
    m = A.mark()
    nb = 4
    xr = Ring([A.alloc([1024], F32) for _ in range(nb)])
    xnr = Ring([A.alloc([1024], BF16) for _ in range(2)])
    junk, t_junk = A.alloc([1024], BF16)
    ssr = Ring([A.alloc([1], F32) for _ in range(nb)])

    def stage_a(i):
        xt, t_x = xr.next()
        load(i, xt, t_x)
        ss, t_ss = ssr.next()
        fw.op(fw.act, lambda e: e.activation(out=junk[:rows], in_=xt[:rows], func=AF.Square, accum_out=ss[:rows]),
              reads=[t_x], writes=[t_junk, t_ss])
        fw.op(fw.act, lambda e: e.activation(out=ss[:rows], in_=ss[:rows], func=AF.Sqrt, scale=1.0 / 1024, bias=EPS),
              reads=[t_ss], writes=[t_ss])
        fw.op(fw.dve, lambda e: e.reciprocal(out=ss[:rows], in_=ss[:rows]), reads=[t_ss], writes=[t_ss])
        return xt, t_x, ss, t_ss

    def stage_b(i, xt, t_x, ss, t_ss):
        xn, t_xn = xnr.next()
        fw.op(fw.act, lambda e: e.activation(out=xn[:rows], in_=xt[:rows], func=AF.Copy, scale=ss[:rows]),
              reads=[t_x, t_ss], writes=[t_xn])
        pb, t_pb = c.psb.next()
        for kc in range(8):
            fw.op(fw.pe, lambda e: e.transpose(out=pb[:, kc, :rows], in_=xn[:rows, kc * 128:(kc + 1) * 128], identity=c.ident[:rows, :rows]),
                  reads=[t_xn, c.t_ident], writes=[t_pb])
        fw.op(fw.dve, lambda e: e.tensor_tensor(out=hT[:, :, col0 + i * rows: col0 + (i + 1) * rows], in0=pb[:, :, :rows],
                                                in1=gT.unsqueeze(2).broadcast_to([128, 8, rows]), op=ALU.mult),
              reads=[t_pb, t_g], writes=[t_h])

    pend = [stage_a(0)]
    if ntiles > 1:
        pend.append(stage_a(1))
    for i in range(ntiles):
        if i + 2 < ntiles:
            pend.append(stage_a(i + 2))
        stage_b(i, *pend.pop(0))
    A.release(m)


def norm_scratch(c):
    A = c.A
    NS = Ctx()
    NS.junk, NS.t_junk = A.alloc([1024], BF16)
    NS.ssr = Ring([A.alloc([1], F32) for _ in range(4)])
    NS.xnr = Ring([A.alloc([1024], BF16) for _ in range(3)])
    return NS


def norm_tile_a(c, NS, xt, t_x):
    fw = c.fw
    ss, t_ss = NS.ssr.next()
    fw.op(fw.act, lambda e: e.activation(out=NS.junk, in_=xt, func=AF.Square, accum_out=ss), reads=[t_x], writes=[NS.t_junk, t_ss])
    fw.op(fw.act, lambda e: e.activation(out=ss, in_=ss, func=AF.Sqrt, scale=1.0 / 1024, bias=EPS), reads=[t_ss], writes=[t_ss])
    fw.op(fw.dve, lambda e: e.reciprocal(out=ss, in_=ss), reads=[t_ss], writes=[t_ss])
    xn, t_xn = NS.xnr.next()
    fw.op(fw.act, lambda e: e.activation(out=xn, in_=xt, func=AF.Copy, scale=ss), reads=[t_x, t_ss], writes=[t_xn])
    return xn, t_xn


def norm_tile_b(c, i, xn, t_xn, gT, t_g, hT, t_h):
    fw = c.fw
    pb, t_pb = c.psb.next()
    for kc in range(8):
        fw.op(fw.pe, lambda e: e.transpose(out=pb[:, kc, :], in_=xn[:, kc * 128:(kc + 1) * 128], identity=c.ident), reads=[t_xn, c.t_ident], writes=[t_pb])
    fw.op(fw.dve, lambda e: e.tensor_tensor(out=hT[:, :, i * 128:(i + 1) * 128], in0=pb[:, :, :], in1=gT.unsqueeze(2).broadcast_to([128, 8, 128]), op=ALU.mult),
          reads=[t_pb, t_g], writes=[t_h])


def wsrc_cols(w_d, col0, ncols):
    return w_d[:, col0:col0 + ncols].rearrange("(kc p) n -> p kc n", p=128)


def proj_fm(c, wr, srcs, actT, t_act, KC, ntt, evac, extra=None):
    fw = c.fw
    for ci, src in enumerate(srcs):
        wb, t_wb = wr.next()
        fw.dma(fw.pool, wb[:, :KC, :], src, writes=[t_wb])
        for tt in range(ntt):
            ps, t_ps = c.ps.next()
            for kc in range(KC):
                fw.op(fw.pe, lambda e, kc=kc, wb=wb, ps=ps, tt=tt: e.matmul(ps[:, :], lhsT=wb[:, kc, :], rhs=actT[:, kc, tt * 512:(tt + 1) * 512],
                                                                           start=(kc == 0), stop=(kc == KC - 1)),
                      reads=[t_wb, t_act], writes=[t_ps])
            evac(ci, tt, ps, t_ps)
        if extra is not None:
            extra(ci, wb, t_wb)


def proj_tm(c, wb, t_wb, ncols, actT, t_act, KC, tiles, evac):
    fw = c.fw
    gsz = 512 // ncols
    for g0 in range(0, len(tiles), gsz):
        grp = tiles[g0:g0 + gsz]
        ps, t_ps = c.ps.next()
        for j, tk in enumerate(grp):
            for kc in range(KC):
                fw.op(fw.pe, lambda e, j=j, tk=tk, kc=kc, ps=ps: e.matmul(ps[:, j * ncols:(j + 1) * ncols], lhsT=actT[:, kc, tk:tk + 128],
                                                                         rhs=wb[:, kc, :ncols], start=(kc == 0), stop=(kc == KC - 1)),
                      reads=[t_wb, t_act], writes=[t_ps])
        evac(g0, len(grp), ps, t_ps)


def hgrn_phase(c, P, hT, t_h, w_in_d, own, S_par, og_d=None):
    fw, A = c.fw, c.A
    B = Ctx()
    B.wr = Ring([A.alloc([8, 128], BF16) for _ in range(3)])
    pb = []
    sgf = A.alloc([T], F32)
    for _ in range(2):
        X = Ctx()
        if own:
            X.qs, X.t_qs = A.alloc([T], BF16)
            X.sgate, X.t_sgate = A.alloc([T], BF16)
        X.sg = [sgf, A.alloc([T], F32)]
        pb.append(X)
    B.v, B.t_v = A.alloc([NT, 128], BF16)
    B.lf, B.t_lf = A.alloc([T], F32)
    B.cum, B.t_cum = A.alloc([T], F32)
    B.tot, B.t_tot = A.alloc([32], F32)
    B.kk, B.t_kk = A.alloc([T], BF16)
    B.e = [A.alloc([T], BF16) for _ in range(2 if own else 1)]
    sb = []
    if own:
        Kt1 = A.alloc([T], BF16)
    Kh1 = A.alloc([T], BF16)
    for _ in range(2):
        X = Ctx()
        if own:
            X.Kt, X.t_Kt = Kt1
            X.Qh, X.t_Qh = A.alloc([T], BF16)
            X.Asb, X.t_Asb = A.alloc([16, 64], BF16)
        X.Kh, X.t_Kh = Kh1
        X.elast, X.t_elast = A.alloc([32], F32)
        sb.append(X)
    for X in sb:
        X.Pn, X.t_Pn = A.alloc([32, 128], BF16)
    for X in sb:
        X.Sall, _t = A.alloc([33, 128], BF16)
        X.Stoks = [Tok() for _ in range(33)]
    if own:
        B.oacc, B.t_oacc = A.alloc([T], F32)
        B.sq = Ring([A.alloc([512], BF16) for _ in range(2)])
        B.rs = Ring([A.alloc([512], F32) for _ in range(2)])
        B.og, B.t_og = A.alloc([T], BF16)
    B.KT = Ring([A.alloc([4, 128], BF16) for _ in range(4)])
    B.seg, B.t_seg = A.alloc([T], BF16)
    fw.op(fw.dve, lambda e: e.memset(B.seg, 1.0), writes=[B.t_seg])
    fw.op(fw.dve, lambda e: e.memset(B.seg.rearrange("p (n c) -> p n c", c=64)[:, :, 0:1], 0.0), writes=[B.t_seg])

    def proj(h, names):
        X = pb[h % 2]
        cols = {"q": C_HQ, "g": C_HG, "ff": C_HFF, "fb": C_HFB}
        fm = [n for n in names if n != "v"]

        def evac(ci, tt, ps, t_ps):
            name = fm[ci]
            sl = slice(tt * 512, (tt + 1) * 512)
            if name == "q":
                fw.op(fw.act, lambda e: e.activation(out=X.qs[:, sl], in_=ps[:, :], func=AF.Silu), reads=[t_ps], writes=[X.t_qs])
            elif name == "g":
                fw.op(fw.act, lambda e: e.activation(out=X.sgate[:, sl], in_=ps[:, :], func=AF.Silu), reads=[t_ps], writes=[X.t_sgate])
            else:
                d = 0 if name == "ff" else 1
                fw.op(fw.act, lambda e: e.activation(out=X.sg[d][0][:, sl], in_=ps[:, :], func=AF.Sigmoid), reads=[t_ps], writes=[X.sg[d][1]])
        if fm:
            proj_fm(c, B.wr, [wsrc_cols(w_in_d, cols[n] + h * 128, 128) for n in fm], hT, t_h, 8, NTT, evac)
        if "v" in names:
            wb, t_wb = B.wr.next()
            fw.dma(fw.pool, wb, wsrc_cols(w_in_d, C_HI + h * 128, 128), writes=[t_wb])

            def evac_v(g0, n, ps, t_ps):
                fw.op(fw.act, lambda e: e.activation(out=B.v[:, g0:g0 + n, :], in_=ps[:, :n * 128].rearrange("p (a b) -> p a b", b=128), func=AF.Copy),
                      reads=[t_ps], writes=[B.t_v])
            proj_tm(c, wb, t_wb, 128, hT, t_h, 8, [i * 128 for i in range(NT)], evac_v)

    def elem(h, d):
        X = pb[h % 2]
        Y = sb[d]
        sg, t_sg = X.sg[d]
        lb = P.lb[:, d, h:h + 1]
        oml = P.oml[:, d, h:h + 1]
        noml = P.noml[:, d, h:h + 1]
        fw.op(fw.act, lambda e: e.activation(out=B.lf, in_=sg, func=AF.Ln, scale=oml, bias=lb), reads=[t_sg, P.t_lb], writes=[B.t_lf])
        fw.op(fw.act, lambda e: e.activation(out=B.kk, in_=sg, func=AF.Identity, scale=noml, bias=oml), reads=[t_sg, P.t_lb], writes=[B.t_kk])
        fw.op(fw.dve, lambda e: e.tensor_scalar(out=B.lf, in0=B.lf, scalar1=LOGF_FLOOR, scalar2=None, op0=ALU.max), reads=[B.t_lf], writes=[B.t_lf])
        fw.op(fw.dve, lambda e: e.tensor_tensor_scan(out=B.cum, data0=B.seg, data1=B.lf, initial=0.0, op0=ALU.mult, op1=ALU.add),
              reads=[B.t_seg, B.t_lf], writes=[B.t_cum])
        v3 = lambda ap: ap.rearrange("p (n c) -> p n c", c=64)
        cumb, t_cumb = B.cum, B.t_cum
        if d == 1:
            fw.op(fw.act, lambda e: e.activation(out=B.tot, in_=v3(B.cum)[:, :, 63], func=AF.Copy), reads=[B.t_cum], writes=[B.t_tot])
            fw.op(fw.dve, lambda e: e.tensor_tensor(out=B.cum, in0=B.lf, in1=B.cum, op=ALU.subtract), reads=[B.t_lf, B.t_cum, B.t_tot], writes=[B.t_cum])
            fw.op(fw.dve, lambda e: e.tensor_tensor(out=v3(B.lf), in0=v3(B.cum), in1=B.tot.unsqueeze(2).broadcast_to([128, 32, 64]), op=ALU.add),
                  reads=[B.t_cum, B.t_tot], writes=[B.t_lf])
            cumb, t_cumb = B.lf, B.t_lf
        c3 = v3(cumb)
        ilast = 63 if d == 0 else 0
        fw.op(fw.act, lambda e: e.activation(out=Y.elast, in_=c3[:, :, ilast], func=AF.Exp), reads=[t_cumb], writes=[Y.t_elast])
        en, t_en = B.e[0]
        fw.op(fw.act, lambda e: e.activation(out=en, in_=cumb, func=AF.Exp, scale=-1.0), reads=[t_cumb], writes=[t_en])
        if own:
            ep, t_ep = B.e[1]
            fw.op(fw.act, lambda e: e.activation(out=ep, in_=cumb, func=AF.Exp), reads=[t_cumb], writes=[t_ep])
            fw.op(fw.dve, lambda e: e.tensor_tensor(out=Y.Kt, in0=B.kk, in1=en, op=ALU.mult), reads=[B.t_kk, t_en], writes=[Y.t_Kt])
            fw.op(fw.dve, lambda e: e.tensor_tensor(out=v3(Y.Kh), in0=v3(Y.Kt), in1=Y.elast.unsqueeze(2).broadcast_to([128, 32, 64]), op=ALU.mult),
                  reads=[Y.t_Kt, Y.t_elast], writes=[Y.t_Kh])
            fw.op(fw.dve, lambda e: e.scalar_tensor_tensor(out=Y.Qh, in0=X.qs, scalar=128 ** -0.5, in1=ep, op0=ALU.mult, op1=ALU.mult),
                  reads=[X.t_qs, t_ep], writes=[Y.t_Qh])
        else:
            fw.op(fw.dve, lambda e: e.tensor_tensor(out=v3(B.tmp), in0=c3, in1=c3[:, :, ilast:ilast + 1].broadcast_to([128, 32, 64]), op=ALU.subtract),
                  reads=[t_cumb], writes=[B.t_tmp])
            fw.op(fw.act, lambda e: e.activation(out=en, in_=B.tmp, func=AF.Exp, scale=-1.0), reads=[B.t_tmp], writes=[t_en])
            fw.op(fw.dve, lambda e: e.tensor_tensor(out=Y.Kh, in0=B.kk, in1=en, op=ALU.mult), reads=[B.t_kk, t_en], writes=[Y.t_Kh])

    def stage1(h, d):
        Y = sb[d]
        kts = []
        for g in range(4):
            n0 = 8 * g
            if own:
                psA, t_psA = c.ps.next()
                pA = psA[:, 0:256].rearrange("p (a b) -> p a b", b=64)
                for j in range(8):
                    n = n0 + j
                    pr, slot = j % 2, j // 2
                    fw.op(fw.pe, lambda e: e.matmul(pA[pr * 64:(pr + 1) * 64, slot, :], lhsT=Y.Kt[:, n * 64:(n + 1) * 64],
                                                    rhs=Y.Qh[:, n * 64:(n + 1) * 64], start=True, stop=True),
                          reads=[Y.t_Kt, Y.t_Qh], writes=[t_psA])
                fw.op(fw.dve, lambda e: e.tensor_tensor(out=Y.Asb[:, 4 * g:4 * g + 4, :], in0=pA, in1=c.tri[:, d, :].unsqueeze(1).broadcast_to([128, 4, 64]), op=ALU.mult),
                      reads=[t_psA, c.t_tri], writes=[Y.t_Asb])
            pb_, t_pb = c.psb.next()
            for slot in range(4):
                tk = (n0 + 2 * slot) * 64
                fw.op(fw.pe, lambda e: e.transpose(out=pb_[:, slot, :], in_=Y.Kh[:, tk:tk + 128], identity=c.ident),
                      reads=[Y.t_Kh, c.t_ident], writes=[t_pb])
            KT, t_KT = B.KT.next()
            fw.op(fw.act, lambda e: e.activation(out=KT, in_=pb_[:, 0:4, :], func=AF.Copy), reads=[t_pb], writes=[t_KT])
            kts.append((KT, t_KT))
        for g in range(4):
            n0 = 8 * g
            KT, t_KT = kts[g]
            psP = [c.ps.next(), c.ps.next()]
            for j in range(8):
                n = n0 + j
                pr, slot = j % 2, j // 2
                pp, t_pp = psP[pr]
                fw.op(fw.pe, lambda e: e.matmul(pp[:, slot * 128:(slot + 1) * 128], lhsT=KT[pr * 64:(pr + 1) * 64, slot, :],
                                                rhs=B.v[pr * 64:(pr + 1) * 64, n // 2, :], start=True, stop=True),
                      reads=[t_KT, B.t_v], writes=[t_pp])
            pn4 = Y.Pn[:, n0:n0 + 8, :].rearrange("p (a two) b -> p a two b", two=2)
            for pr in range(2):
                pp, t_pp = psP[pr]
                fw.op(fw.act, lambda e: e.activation(out=pn4[:, :, pr, :], in_=pp[:, :].rearrange("p (a b) -> p a b", b=128), func=AF.Copy),
                      reads=[t_pp], writes=[Y.t_Pn])

    def stage2(h, d):
        Y = sb[d]
        if own:
            s_ap, s_tok = S_par[d][0][:, h, :], S_par[d][1]
            fw.op(fw.dve, lambda e: e.tensor_copy(out=Y.Sall[:, 0, :], in_=s_ap), reads=[s_tok], writes=[Y.Stoks[0]])
        else:
            fw.op(fw.dve, lambda e: e.memset(Y.Sall[:, 0, :], 0.0), writes=[Y.Stoks[0]])
        for i in range(32):
            n = i if d == 0 else 31 - i
            fw.op(fw.dve, lambda e: e.scalar_tensor_tensor(out=Y.Sall[:, i + 1, :], in0=Y.Sall[:, i, :], scalar=Y.elast[:, n:n + 1], in1=Y.Pn[:, n, :],
                                                           op0=ALU.mult, op1=ALU.add),
                  reads=[Y.Stoks[i], Y.t_elast, Y.t_Pn], writes=[Y.Stoks[i + 1]])
        if not own:
            so_ap, so_tok = S_par[d][0][:, h, :], S_par[d][1]
            fw.op(fw.dve, lambda e: e.tensor_scalar(out=so_ap, in0=Y.Sall[:, 32, :], scalar1=P.sel[:, d:d + 1], scalar2=None, op0=ALU.mult),
                  reads=[Y.Stoks[32], P.t_sel], writes=[so_tok])

    def stage3(h, d):
        Y = sb[d]
        for g in range(4):
            n0 = 8 * g
            psO = [c.ps.next(), c.ps.next()]
            for j in range(8):
                n = n0 + j
                i = n if d == 0 else 31 - n
                pr, slot = j % 2, j // 2
                po, t_po = psO[pr]
                fw.op(fw.pe, lambda e: e.matmul(po[:, slot * 64:(slot + 1) * 64], lhsT=B.v[pr * 64:(pr + 1) * 64, n // 2, :],
                                                rhs=Y.Asb[pr * 64:(pr + 1) * 64, n // 2, :], start=True, stop=False),
                      reads=[B.t_v, Y.t_Asb], writes=[t_po])
                fw.op(fw.pe, lambda e: e.matmul(po[:, slot * 64:(slot + 1) * 64], lhsT=Y.Sall[:, i, :], rhs=Y.Qh[:, n * 64:(n + 1) * 64], start=False, stop=True),
                      reads=[Y.Stoks[i], Y.t_Qh], writes=[t_po])
            oa = B.oacc[:, n0 * 64:(n0 + 8) * 64].rearrange("p (a two b) -> p a two b", two=2, b=64)
            for pr in range(2):
                po, t_po = psO[pr]
                src = po[:, 0:256].rearrange("p (a b) -> p a b", b=64)
                if d == 0:
                    fw.op(fw.act, lambda e: e.activation(out=oa[:, :, pr, :], in_=src, func=AF.Copy), reads=[t_po], writes=[B.t_oacc])
                else:
                    fw.op(fw.dve, lambda e: e.tensor_tensor(out=oa[:, :, pr, :], in0=oa[:, :, pr, :], in1=src, op=ALU.add),
                          reads=[t_po, B.t_oacc], writes=[B.t_oacc])

    def norm(h):
        X = pb[h % 2]
        for tt in range(NTT):
            sl = slice(tt * 512, (tt + 1) * 512)
            sq, t_sq = B.sq.next()
            fw.op(fw.act, lambda e: e.activation(out=sq, in_=B.oacc[:, sl], func=AF.Square), reads=[B.t_oacc], writes=[t_sq])
            ps, t_ps = c.ps.next()
            fw.op(fw.pe, lambda e: e.matmul(ps[:, :], lhsT=c.onesd, rhs=sq, start=True, stop=True), reads=[c.t_onesd, t_sq], writes=[t_ps])
            rs, t_rs = B.rs.next()
            fw.op(fw.act, lambda e: e.activation(out=rs, in_=ps[:, :], func=AF.Ln, bias=P.epsc[:, 0:1]), reads=[t_ps, P.t_epsc], writes=[t_rs])
            fw.op(fw.act, lambda e: e.activation(out=rs, in_=rs, func=AF.Exp, scale=-0.5), reads=[t_rs], writes=[t_rs])
            fw.op(fw.dve, lambda e: e.tensor_tensor(out=rs, in0=rs, in1=B.oacc[:, sl], op=ALU.mult), reads=[t_rs, B.t_oacc], writes=[t_rs])
            fw.op(fw.dve, lambda e: e.scalar_tensor_tensor(out=B.og[:, sl], in0=rs, scalar=P.gnorm[:, 0:1], in1=X.sgate[:, sl], op0=ALU.mult, op1=ALU.mult),
                  reads=[t_rs, P.t_gnorm, X.t_sgate], writes=[B.t_og])
        fw.dma(fw.sp, og_d[h * 128:(h + 1) * 128, :], B.og, reads=[B.t_og])

    nxt = lambda h, names: proj(h + 1, [n for n in names if own or n in ("ff", "fb")]) if h + 1 < 8 else None
    proj(0, ["q", "ff", "fb", "g"])
    for h in range(8):
        elem(h, 0)
        proj(h, ["v"])
        if h > 0:
            norm(h - 1)
        nxt(h, ["q"])
        stage1(h, 0)
        elem(h, 1)
        nxt(h, ["ff"])
        stage1(h, 1)
        stage2(h, 0)
        nxt(h, ["fb"])
        stage2(h, 1)
        stage3(h, 0)
        nxt(h, ["g"])
        stage3(h, 1)
    norm(7)


def hgrn_partner(c, P, hT, t_h, w_in_d, S_par):
    fw, A = c.fw, c.A
    wr = Ring([A.alloc([8, 128], BF16) for _ in range(3)])
    sgr = [Ring([A.alloc([T], F32) for _ in range(2)]) for _ in range(2)]
    vr = Ring([A.alloc([NT, 128], BF16) for _ in range(2)])
    lfr = Ring([A.alloc([T], F32) for _ in range(2)])
    cumr = Ring([A.alloc([T], F32) for _ in range(2)])
    kkr = Ring([A.alloc([T], BF16) for _ in range(2)])
    er = Ring([A.alloc([T], BF16) for _ in range(2)])
    Khr = Ring([A.alloc([T], BF16) for _ in range(2)])
    KTr = Ring([A.alloc([4, 128], BF16) for _ in range(4)])
    onesT, t_onesT = A.alloc([T], BF16)
    fw.op(fw.dve, lambda e: e.memset(onesT, 1.0), writes=[t_onesT])

    def proj_gate(h, d):
        sg, t_sg = sgr[d].next()

        def evac(ci, tt, ps, t_ps):
            fw.op(fw.act, lambda e: e.activation(out=sg[:, tt * 512:(tt + 1) * 512], in_=ps[:, :], func=AF.Sigmoid), reads=[t_ps], writes=[t_sg])
        proj_fm(c, wr, [wsrc_cols(w_in_d, (C_HFF if d == 0 else C_HFB) + h * 128, 128)], hT, t_h, 8, NTT, evac)
        return sg, t_sg

    def proj_v(h):
        wb, t_wb = wr.next()
        fw.dma(fw.pool, wb, wsrc_cols(w_in_d, C_HI + h * 128, 128), writes=[t_wb])
        v, t_v = vr.next()

        def evac_v(g0, n, ps, t_ps):
            fw.op(fw.act, lambda e: e.activation(out=v[:, g0:g0 + n, :], in_=ps[:, :n * 128].rearrange("p (a b) -> p a b", b=128), func=AF.Copy),
                  reads=[t_ps], writes=[t_v])
        proj_tm(c, wb, t_wb, 128, hT, t_h, 8, [i * 128 for i in range(NT)], evac_v)
        return v, t_v

    def st_elem(h, d, sg, t_sg):
        lb = P.lb[:, d, h:h + 1]
        oml = P.oml[:, d, h:h + 1]
        noml = P.noml[:, d, h:h + 1]
        lf, t_lf = lfr.next()
        cum, t_cum = cumr.next()
        kk, t_kk = kkr.next()
        ex, t_ex = er.next()
        Kh, t_Kh = Khr.next()
        fw.op(fw.act, lambda e: e.activation(out=lf, in_=sg, func=AF.Ln, scale=oml, bias=lb), reads=[t_sg, P.t_lb], writes=[t_lf])
        fw.op(fw.act, lambda e: e.activation(out=kk, in_=sg, func=AF.Identity, scale=noml, bias=oml), reads=[t_sg, P.t_lb], writes=[t_kk])
        fw.op(fw.dve, lambda e: e.tensor_scalar(out=lf, in0=lf, scalar1=LOGF_FLOOR, scalar2=None, op0=ALU.max), reads=[t_lf], writes=[t_lf])
        fw.op(fw.dve, lambda e: e.tensor_tensor_scan(out=cum, data0=onesT, data1=lf, initial=0.0, op0=ALU.mult, op1=ALU.add),
              reads=[t_onesT, t_lf], writes=[t_cum])
        if d == 0:
            fw.op(fw.act, lambda e: e.activation(out=ex, in_=cum, func=AF.Exp, scale=-1.0, bias=cum[:, T - 1:T]), reads=[t_cum], writes=[t_ex])
        else:
            fw.op(fw.dve, lambda e: e.tensor_tensor(out=lf, in0=cum, in1=lf, op=ALU.subtract), reads=[t_cum, t_lf], writes=[t_lf])
            fw.op(fw.act, lambda e: e.activation(out=ex, in_=lf, func=AF.Exp), reads=[t_lf], writes=[t_ex])
        fw.op(fw.dve, lambda e: e.tensor_tensor(out=Kh, in0=kk, in1=ex, op=ALU.mult), reads=[t_kk, t_ex], writes=[t_Kh])
        return Kh, t_Kh

    def st_mm(h, d, Kh, t_Kh, v, t_v):
        kts = []
        for g in range(4):
            pb_, t_pb = c.psb.next()
            for slot in range(4):
                tk = (4 * g + slot) * 128
                fw.op(fw.pe, lambda e: e.transpose(out=pb_[:, slot, :], in_=Kh[:, tk:tk + 128], identity=c.ident), reads=[t_Kh, c.t_ident], writes=[t_pb])
            KT, t_KT = KTr.next()
            fw.op(fw.act, lambda e: e.activation(out=KT, in_=pb_[:, 0:4, :], func=AF.Copy), reads=[t_pb], writes=[t_KT])
            kts.append((KT, t_KT))
        ps, t_ps = c.ps.next()
        for g in range(4):
            KT, t_KT = kts[g]
            for slot in range(4):
                i = 4 * g + slot
                fw.op(fw.pe, lambda e: e.matmul(ps[:, 0:128], lhsT=KT[:, slot, :], rhs=v[:, i, :], start=(i == 0), stop=(i == NT - 1)),
                      reads=[t_KT, t_v], writes=[t_ps])
        so_ap, so_tok = S_par[d][0][:, h, :], S_par[d][1]
        fw.op(fw.dve, lambda e: e.tensor_scalar(out=so_ap, in0=ps[:, 0:128], scalar1=P.sel[:, d:d + 1], scalar2=None, op0=ALU.mult),
              reads=[t_ps, P.t_sel], writes=[so_tok])

    sg0, sg1, vv_ = proj_gate(0, 0), proj_gate(0, 1), proj_v(0)
    for h in range(8):
        k0 = st_elem(h, 0, *sg0)
        if h + 1 < 8:
            n0 = proj_gate(h + 1, 0)
        k1 = st_elem(h, 1, *sg1)
        if h + 1 < 8:
            n1 = proj_gate(h + 1, 1)
        st_mm(h, 0, *k0, *vv_)
        st_mm(h, 1, *k1, *vv_)
        if h + 1 < 8:
            vv_ = proj_v(h + 1)
            sg0, sg1 = n0, n1


def na_tiles(j):
    lo = min(j, 28)
    hi = max(j, 4) + 8
    return list(range(lo // 2, (hi - 1) // 2 + 1))


def na_slot_base():
    base = {}
    nxt = 9
    for j in range(32):
        if 4 <= j <= 28:
            base[j] = 0 if j % 2 == 0 else 4
        else:
            base[j] = nxt
            nxt += len(na_tiles(j))
    assert nxt == NSLOT
    return base


def emit_mixer(c, I):
    fw, A = c.fw, c.A
    m0 = A.mark()
    P = Ctx()
    gmix, t_gmix = load_vecT(c, I.norm_mix, 8)
    gmem, t_gmem = load_vecT(c, I.mem_norm, 8)
    P.gnorm, P.t_gnorm = A.alloc([1], F32)
    fw.dma(fw.sp, P.gnorm, I.gnorm.rearrange("(p o) -> p o", o=1), writes=[P.t_gnorm])
    P.sel, P.t_sel = A.alloc([2], F32)
    fw.dma(fw.sp, P.sel, I.sel, writes=[P.t_sel])
    P.epsc, P.t_epsc = A.alloc([1], F32)
    fw.op(fw.dve, lambda e: e.memset(P.epsc, EPS), writes=[P.t_epsc])
    lbsel, t_lbsel = A.alloc([DEPTH], F32)
    fw.dma(fw.sp, lbsel, I.lbsel, writes=[t_lbsel])
    lg, t_lg = A.alloc([DEPTH, 2, 8], F32)
    for l in range(DEPTH):
        for d in range(2):
            fw.dma(fw.sp, lg[:, l, d, :], I.lb_logits[l, d, :].rearrange("(h k) -> k h", k=128), writes=[t_lg], allow_slow_non_contiguous=True)
    fw.op(fw.act, lambda e: e.activation(out=lg, in_=lg, func=AF.Exp), reads=[t_lg], writes=[t_lg])
    P.lb, P.t_lb = A.alloc([2, 8], F32)
    P.oml, _ = A.alloc([2, 8], F32)
    P.noml, _ = A.alloc([2, 8], F32)
    tot, t_tot = A.alloc([2, 8], F32)
    fw.op(fw.dve, lambda e: e.tensor_tensor(out=tot, in0=lg[:, 0], in1=lg[:, 1], op=ALU.add), reads=[t_lg], writes=[t_tot])
    fw.op(fw.dve, lambda e: e.reciprocal(out=tot, in_=tot), reads=[t_tot], writes=[t_tot])
    fw.op(fw.dve, lambda e: e.tensor_scalar(out=P.lb, in0=lg[:, 0], scalar1=lbsel[:, 0:1], scalar2=None, op0=ALU.mult), reads=[t_lg, t_lbsel], writes=[P.t_lb])
    fw.op(fw.dve, lambda e: e.scalar_tensor_tensor(out=P.lb, in0=lg[:, 1], scalar=lbsel[:, 1:2], in1=P.lb, op0=ALU.mult, op1=ALU.add),
          reads=[t_lg, t_lbsel, P.t_lb], writes=[P.t_lb])
    fw.op(fw.dve, lambda e: e.tensor_tensor(out=P.lb, in0=P.lb, in1=tot, op=ALU.mult), reads=[P.t_lb, t_tot], writes=[P.t_lb])
    fw.op(fw.dve, lambda e: e.tensor_scalar(out=P.lb, in0=P.lb, scalar1=0.0, scalar2=1.0, op0=ALU.max, op1=ALU.min), reads=[P.t_lb], writes=[P.t_lb])
    fw.op(fw.dve, lambda e: e.tensor_scalar(out=P.oml, in0=P.lb, scalar1=-1.0, scalar2=1.0, op0=ALU.mult, op1=ALU.add), reads=[P.t_lb], writes=[P.t_lb])
    fw.op(fw.dve, lambda e: e.tensor_scalar(out=P.noml, in0=P.lb, scalar1=1.0, scalar2=-1.0, op0=ALU.mult, op1=ALU.add), reads=[P.t_lb], writes=[P.t_lb])
    S_par = [A.alloc([8, 128], BF16) for _ in range(2)]

    hT, t_h = I.HT[:, :, 0:T], I.t_HT
    if not I.h_ready:
        rmsnorm_T(c, I.x_load, NT, gmix, t_gmix, hT, t_h)
    m_ca = A.mark()
    build_ca(c, hT, t_h, I.mem, gmem, t_gmem, I.w_in, I.w_mem_kv, I.oca_s)
    A.release(m_ca)
    m_hp = A.mark()
    hpT, t_hp = A.alloc([8, T], BF16)
    rmsnorm_T(c, I.xp_load, NT, gmix, t_gmix, hpT, t_hp)

    m_na = A.mark()
    build_na(c, hT, t_h, hpT, t_hp, I.w_in, I.na_bias, I.ona_s)
    A.release(m_na)

    m2 = A.mark()
    hgrn_partner(c, P, hpT, t_hp, I.w_in, S_par)
    A.release(m2)
    A.release(m_hp)

    m5 = A.mark()
    hgrn_phase(c, P, hT, t_h, I.w_in, True, S_par, og_d=I.og_s)
    A.release(m5)

    build_tail(c, hT, t_h, I.x_load, I.x_store, I.w_in, I.og_s, I.ona_s, I.oca_s, I.w_hg_o, I.w_na_o, I.w_ca_o, I.w_out, I)
    A.release(m0)


def build_na(c, hT, t_h, hpT, t_hp, w_in_d, nab_d, ona_d):
    fw, A = c.fw, c.A
    wr = Ring([A.alloc([8, 128], BF16) for _ in range(3)])
    qbd, t_q = A.alloc([4, 32, 128], BF16)
    kT, t_k = A.alloc([4, 2560], BF16)
    vv, t_v = A.alloc([20, 512], BF16)
    onar = Ring([A.alloc([T], BF16) for _ in range(2)])
    fw.op(fw.dve, lambda e: e.memset(qbd, 0.0), writes=[t_q])

    def evq(ci, tt, ps, t_ps):
        for a in range(2):
            pa = slice(a * 64, (a + 1) * 64)
            fw.op(fw.act, lambda e: e.activation(out=qbd[pa, ci, tt * 8:(tt + 1) * 8, a * 64:(a + 1) * 64], in_=ps[pa, :].rearrange("p (j q) -> p j q", q=64),
                                                 func=AF.Copy, scale=0.125), reads=[t_ps], writes=[t_q])
    proj_fm(c, wr, [wsrc_cols(w_in_d, C_NQ + i * 128, 128) for i in range(4)], hT, t_h, 8, NTT, evq)

    def evk(ci, tt, ps, t_ps):
        fw.op(fw.act, lambda e: e.activation(out=kT[:, ci, 256 + tt * 512:256 + (tt + 1) * 512], in_=ps[:, :], func=AF.Copy), reads=[t_ps], writes=[t_k])

    def halo_k(ci, wb, t_wb):
        ps, t_ps = c.ps.next()
        for half, tk in enumerate((1792, 0)):
            for kc in range(8):
                fw.op(fw.pe, lambda e: e.matmul(ps[:, half * 256:(half + 1) * 256], lhsT=wb[:, kc, :], rhs=hpT[:, kc, tk:tk + 256],
                                                start=(kc == 0), stop=(kc == 7)), reads=[t_wb, t_hp], writes=[t_ps])
        fw.op(fw.act, lambda e: e.activation(out=kT[:, ci, 0:256], in_=ps[:, 0:256], func=AF.Copy), reads=[t_ps], writes=[t_k])
        fw.op(fw.act, lambda e: e.activation(out=kT[:, ci, 2304:2560], in_=ps[:, 256:512], func=AF.Copy), reads=[t_ps], writes=[t_k])
    proj_fm(c, wr, [wsrc_cols(w_in_d, C_NK + i * 128, 128) for i in range(4)], hT, t_h, 8, NTT, evk, extra=halo_k)
    wv, t_wv = A.alloc([8, 512], BF16)
    fw.dma(fw.pool, wv, wsrc_cols(w_in_d, C_NV, 512), writes=[t_wv])

    def evv(off):
        def f(g0, n, ps, t_ps):
            fw.op(fw.act, lambda e: e.activation(out=vv[:, off + g0, :], in_=ps[:, :], func=AF.Copy), reads=[t_ps], writes=[t_v])
        return f
    proj_tm(c, wv, t_wv, 512, hT, t_h, 8, [i * 128 for i in range(NT)], evv(2))
    proj_tm(c, wv, t_wv, 512, hpT, t_hp, 8, [1792, 1920], evv(0))
    proj_tm(c, wv, t_wv, 512, hpT, t_hp, 8, [0, 128], evv(18))

    sbase = na_slot_base()
    Etab = Ring([A.alloc([NSLOT, 128], BF16) for _ in range(2)])
    Per = Ring([A.alloc([6, 128], BF16) for _ in range(2)])
    Ptr = Ring([A.alloc([6, 128], BF16) for _ in range(2)])
    rc, t_rc = A.alloc([4, 64], F32)
    po, t_po = c.pacc[0]
    pd, t_pd = c.pacc[1]
    pd3 = pd[:, :].rearrange("p (j q) -> p j q", q=128)
    po3 = po[:, 0:256].rearrange("p (j q) -> p j q", q=64)
    for hp in range(4):
        onaT, t_ona = onar.next()
        E, t_E = Etab.next()
        fw.dma(fw.pool, E, nab_d[hp], writes=[t_E])
        fw.op(fw.act, lambda e: e.activation(out=E, in_=E, func=AF.Exp), reads=[t_E], writes=[t_E])

        def scores(j):
            tiles = na_tiles(j)
            banks = [c.ps.next() for _ in range((len(tiles) + 3) // 4)]
            for s, p in enumerate(tiles):
                ps, t_ps = banks[s // 4]
                fw.op(fw.pe, lambda e: e.matmul(ps[:, (s % 4) * 128:(s % 4 + 1) * 128], lhsT=kT[:, hp, p * 128:(p + 1) * 128], rhs=qbd[:, hp, j, :],
                                                start=True, stop=True), reads=[t_k, t_q], writes=[t_ps])
            Pe, t_Pe = Per.next()
            for bi, (ps, t_ps) in enumerate(banks):
                n = min(4, len(tiles) - 4 * bi)
                fw.op(fw.act, lambda e: e.activation(out=Pe[:, 4 * bi:4 * bi + n, :], in_=ps[:, :n * 128].rearrange("p (s q) -> p s q", q=128), func=AF.Exp),
                      reads=[t_ps], writes=[t_Pe])
            Pt, t_Pt = Ptr.next()
            nt = len(tiles)
            fw.op(fw.dve, lambda e: e.tensor_tensor(out=Pt[:, :nt, :], in0=Pe[:, :nt, :], in1=E[:, sbase[j]:sbase[j] + nt, :], op=ALU.mult),
                  reads=[t_Pe, t_E], writes=[t_Pt])
            return Pt, t_Pt

        def pv(j, Pt, t_Pt):
            tiles = na_tiles(j)
            nt = len(tiles)
            jj = j % 4
            for a in range(2):
                pa = slice(a * 64, (a + 1) * 64)
                hd = 2 * hp + a
                for s, p in enumerate(tiles):
                    fw.op(fw.pe, lambda e: e.matmul(po[pa, jj * 64:(jj + 1) * 64], lhsT=vv[:, p, hd * 64:(hd + 1) * 64], rhs=Pt[:, s, a * 64:(a + 1) * 64],
                                                    start=(s == 0), stop=(s == nt - 1)), reads=[t_v, t_Pt], writes=[t_po])
            for s, p in enumerate(tiles):
                fw.op(fw.pe, lambda e: e.matmul(pd[:, jj * 128:(jj + 1) * 128], lhsT=c.ones, rhs=Pt[:, s, :], start=(s == 0), stop=(s == nt - 1)),
                      reads=[c.t_ones, t_Pt], writes=[t_pd])
            if jj == 3:
                jb = j // 4
                for a in range(2):
                    pa = slice(a * 64, (a + 1) * 64)
                    fw.op(fw.act, lambda e: e.activation(out=rc[pa], in_=pd3[pa, :, a * 64:(a + 1) * 64], func=AF.Ln), reads=[t_pd], writes=[t_rc])
                    fw.op(fw.act, lambda e: e.activation(out=rc[pa], in_=rc[pa], func=AF.Exp, scale=-1.0), reads=[t_rc], writes=[t_rc])
                    fw.op(fw.dve, lambda e: e.tensor_tensor(out=onaT[pa, jb * 256:(jb + 1) * 256].rearrange("p (j q) -> p j q", q=64), in0=po3[pa], in1=rc[pa], op=ALU.mult),
                          reads=[t_po, t_rc], writes=[t_ona])

        cur = scores(0)
        for j in range(32):
            nxt = scores(j + 1) if j + 1 < 32 else None
            pv(j, *cur)
            cur = nxt
        fw.dma(fw.sp, ona_d[hp * 128:(hp + 1) * 128, :], onaT, reads=[t_ona])


def build_ca(c, hT, t_h, mem_d, gmem, t_gmem, w_in_d, w_kv_d, oca_d):
    fw, A = c.fw, c.A
    memT, t_mem = A.alloc([8, 256], BF16)
    rmsnorm_T(c, dram_loader(c, mem_d), 2, gmem, t_gmem, memT, t_mem)
    wr = Ring([A.alloc([8, 128], BF16) for _ in range(3)])
    kcT, t_kc = A.alloc([4, 256], BF16)
    vca, t_vca = A.alloc([2, 512], BF16)
    ocaT, t_oca = A.alloc([4, T], BF16)
    for hh in range(4):
        wb, t_wb = wr.next()
        fw.dma(fw.pool, wb, wsrc_cols(w_kv_d, hh * 128, 128), writes=[t_wb])
        ps, t_ps = c.ps.next()
        for kc in range(8):
            fw.op(fw.pe, lambda e, kc=kc, wb=wb, ps=ps: e.matmul(ps[:, 0:256], lhsT=wb[:, kc, :], rhs=memT[:, kc, :], start=(kc == 0), stop=(kc == 7)),
                  reads=[t_wb, t_mem], writes=[t_ps])
        fw.op(fw.act, lambda e, ps=ps, hh=hh: e.activation(out=kcT[:, hh, :], in_=ps[:, 0:256], func=AF.Copy), reads=[t_ps], writes=[t_kc])
    wv, t_wv = A.alloc([8, 512], BF16)
    fw.dma(fw.pool, wv, wsrc_cols(w_kv_d, 512, 512), writes=[t_wv])

    def evv(g0, n, ps, t_ps):
        fw.op(fw.act, lambda e: e.activation(out=vca[:, g0, :], in_=ps[:, :], func=AF.Copy), reads=[t_ps], writes=[t_vca])
    proj_tm(c, wv, t_wv, 512, memT, t_mem, 8, [0, 128], evv)
    cq4, t_cq = A.alloc([NTT, 512], BF16)
    Pt8, t_Pt = A.alloc([NTT * 2, 512], BF16)
    rcr = Ring([A.alloc([512], F32) for _ in range(2)])
    for hh in range(4):
        wb, t_wb = wr.next()
        fw.dma(fw.pool, wb, wsrc_cols(w_in_d, C_CQ + hh * 128, 128), writes=[t_wb])
        for tt in range(NTT):
            ps, t_ps = c.ps.next()
            for kc in range(8):
                fw.op(fw.pe, lambda e: e.matmul(ps[:, :], lhsT=wb[:, kc, :], rhs=hT[:, kc, tt * 512:(tt + 1) * 512], start=(kc == 0), stop=(kc == 7)),
                      reads=[t_wb, t_h], writes=[t_ps])
            fw.op(fw.act, lambda e: e.activation(out=cq4[:, tt, :], in_=ps[:, :], func=AF.Copy, scale=128 ** -0.5), reads=[t_ps], writes=[t_cq])
        for tt in range(NTT):
            for mt in range(2):
                ps2, t_ps2 = c.ps.next()
                fw.op(fw.pe, lambda e: e.matmul(ps2[:, :], lhsT=kcT[:, hh, mt * 128:(mt + 1) * 128], rhs=cq4[:, tt, :], start=True, stop=True),
                      reads=[t_kc, t_cq], writes=[t_ps2])
                fw.op(fw.act, lambda e: e.activation(out=Pt8[:, tt * 2 + mt, :], in_=ps2[:, :], func=AF.Exp), reads=[t_ps2], writes=[t_Pt])
        for tt in range(NTT):
            po, t_po = c.ps.next()
            pd, t_pd = c.ps.next()
            for mt in range(2):
                fw.op(fw.pe, lambda e: e.matmul(po[:, :], lhsT=vca[:, mt, hh * 128:(hh + 1) * 128], rhs=Pt8[:, tt * 2 + mt, :], start=(mt == 0), stop=(mt == 1)),
                      reads=[t_vca, t_Pt], writes=[t_po])
            for mt in range(2):
                fw.op(fw.pe, lambda e: e.matmul(pd[:, :], lhsT=c.ones, rhs=Pt8[:, tt * 2 + mt, :], start=(mt == 0), stop=(mt == 1)),
                      reads=[c.t_ones, t_Pt], writes=[t_pd])
            rc, t_rc = rcr.next()
            fw.op(fw.dve, lambda e: e.reciprocal(out=rc, in_=pd[:, :]), reads=[t_pd], writes=[t_rc])
            fw.op(fw.dve, lambda e: e.tensor_tensor(out=ocaT[:, hh, tt * 512:(tt + 1) * 512], in0=po[:, :], in1=rc, op=ALU.mult),
                  reads=[t_po, t_rc], writes=[t_oca])
    fw.dma(fw.sp, oca_d.rearrange("(c p) t -> p c t", p=128), ocaT, reads=[t_oca])


def build_tail(c, hT, t_h, x_load, x_store, w_in_d, og_d, ona_d, oca_d, w_hg_o_d, w_na_o_d, w_ca_o_d, w_out_d, I):
    fw, A = c.fw, c.A
    fw.barrier()
    ogT, t_og = A.alloc([8, T], BF16)
    onaT, t_ona = A.alloc([4, T], BF16)
    ocaT, t_oca = A.alloc([4, T], BF16)
    fw.dma(fw.sp, ogT, og_d.rearrange("(c p) t -> p c t", p=128), writes=[t_og])
    fw.dma(fw.sp, onaT, ona_d.rearrange("(c p) t -> p c t", p=128), writes=[t_ona])
    fw.dma(fw.sp, ocaT, oca_d.rearrange("(c p) t -> p c t", p=128), writes=[t_oca])
    mT, t_m = A.alloc([8, T], BF16)
    wo = Ring([A.alloc([8, 128], BF16) for _ in range(4)])
    wg = Ring([A.alloc([8, 128], BF16) for _ in range(4)])
    sgr = Ring([A.alloc([512], F32) for _ in range(2)])
    accr = Ring([A.alloc([512], F32) for _ in range(2)])
    tmpr = Ring([A.alloc([512], F32) for _ in range(2)])
    branches = [(ogT, t_og, 8, w_hg_o_d, C_GHG), (onaT, t_ona, 4, w_na_o_d, C_GNA), (ocaT, t_oca, 4, w_ca_o_d, C_GCA)]
    for m in range(8):
        ws = []
        for (oT, t_o, KC, wod, gcol) in branches:
            wb, t_wb = wo.next()
            fw.dma(fw.pool, wb[:, :KC, :], wod[:, m * 128:(m + 1) * 128].rearrange("(kc p) n -> p kc n", p=128), writes=[t_wb])
            wgb, t_wgb = wg.next()
            fw.dma(fw.pool, wgb, wsrc_cols(w_in_d, gcol + m * 128, 128), writes=[t_wgb])
            ws.append((wb, t_wb, wgb, t_wgb))
        for tt in range(NTT):
            sl = slice(tt * 512, (tt + 1) * 512)
            acc, t_acc = accr.next()
            for bi, (oT, t_o, KC, wod, gcol) in enumerate(branches):
                wb, t_wb, wgb, t_wgb = ws[bi]
                psy, t_psy = c.ps.next()
                for kc in range(KC):
                    fw.op(fw.pe, lambda e, kc=kc, wb=wb, psy=psy, oT=oT, KC=KC: e.matmul(psy[:, :], lhsT=wb[:, kc, :], rhs=oT[:, kc, sl], start=(kc == 0), stop=(kc == KC - 1)),
                          reads=[t_wb, t_o], writes=[t_psy])
                psg, t_psg = c.ps.next()
                for kc in range(8):
                    fw.op(fw.pe, lambda e, kc=kc, wgb=wgb, psg=psg: e.matmul(psg[:, :], lhsT=wgb[:, kc, :], rhs=hT[:, kc, sl], start=(kc == 0), stop=(kc == 7)),
                          reads=[t_wgb, t_h], writes=[t_psg])
                sg, t_sg = sgr.next()
                fw.op(fw.act, lambda e, psg=psg, sg=sg: e.activation(out=sg, in_=psg[:, :], func=AF.Sigmoid), reads=[t_psg], writes=[t_sg])
                if bi == 0:
                    fw.op(fw.dve, lambda e, psy=psy, sg=sg, acc=acc: e.tensor_tensor(out=acc, in0=psy[:, :], in1=sg, op=ALU.mult), reads=[t_psy, t_sg], writes=[t_acc])
                else:
                    tmp, t_tmp = tmpr.next()
                    fw.op(fw.dve, lambda e, psy=psy, sg=sg, tmp=tmp: e.tensor_tensor(out=tmp, in0=psy[:, :], in1=sg, op=ALU.mult), reads=[t_psy, t_sg], writes=[t_tmp])
                    if bi == 1:
                        fw.op(fw.dve, lambda e, tmp=tmp, acc=acc: e.tensor_tensor(out=acc, in0=acc, in1=tmp, op=ALU.add), reads=[t_tmp, t_acc], writes=[t_acc])
                    else:
                        fw.op(fw.dve, lambda e, tmp=tmp, acc=acc, m=m: e.tensor_tensor(out=mT[:, m, sl], in0=acc, in1=tmp, op=ALU.add), reads=[t_tmp, t_acc], writes=[t_m])
    gffn, t_gffn = load_vecT(c, I.norm_ffn, 8)
    wout, t_wout = A.alloc([8, 1024], BF16)
    for half in range(2):
        fw.dma(fw.pool, wout[:, :, half * 512:(half + 1) * 512], wsrc_cols(w_out_d, half * 512, 512), writes=[t_wout])
    xr = Ring([A.alloc([1024], F32) for _ in range(3)])
    pend = None
    for i in range(NT):
        xt, t_x = xr.next()
        x_load(i, xt, t_x)
        for half in range(2):
            ps, t_ps = c.ps.next()
            for kc in range(8):
                fw.op(fw.pe, lambda e, kc=kc, ps=ps, half=half, i=i: e.matmul(ps[:, :], lhsT=mT[:, kc, i * 128:(i + 1) * 128], rhs=wout[:, kc, half * 512:(half + 1) * 512],
                                                                             start=(kc == 0), stop=(kc == 7)), reads=[t_m, t_wout], writes=[t_ps])
            fw.op(fw.dve, lambda e, ps=ps, xt=xt, half=half: e.tensor_tensor(out=xt[:, half * 512:(half + 1) * 512], in0=xt[:, half * 512:(half + 1) * 512], in1=ps[:, :], op=ALU.add),
                  reads=[t_ps, t_x], writes=[t_x])
        x_store(i, xt, t_x)
        if pend is not None:
            norm_tile_b(c, i - 1, *pend, gffn, t_gffn, hT, t_h)
        pend = norm_tile_a(c, I.NS, xt, t_x)
    norm_tile_b(c, NT - 1, *pend, gffn, t_gffn, hT, t_h)


def emit_ffn(c, I):
    if True:
        fw, A = c.fw, c.A
        m0 = A.mark()
        norm_d, w_up_d, conv_w_d, conv_b_d, w_down_d, nf_d = I.norm_ffn, I.w_up, I.conv_w, I.conv_b, I.w_down, I.norm_final
        gffn, t_gffn = load_vecT(c, norm_d, 8)
        cw, t_cw = A.alloc([3, 44], F32)
        for tap in range(3):
            fw.dma(fw.sp, cw[:, tap, :], conv_w_d[tap, :].rearrange("(c p) -> p c", p=128), writes=[t_cw], allow_slow_non_contiguous=True)
        cb, t_cb = load_vecT(c, conv_b_d, 44)
        actT, t_act = A.alloc([NFF, T], BF16)
        m1 = A.mark()
        h2T, t_h2 = I.HT, I.t_HT
        rmsnorm_T(c, I.halo_load, 1, gffn, t_gffn, h2T, t_h2, col0=T, rows=2)
        wr = Ring([A.alloc([8, 128], BF16) for _ in range(4)])
        ur = Ring([A.alloc([T + 2], F32) for _ in range(2)])
        ucr = Ring([A.alloc([T], F32) for _ in range(2)])
        x2r = Ring([A.alloc([T], F32) for _ in range(2)])
        for j in range(NFF):
            x2, t_x2 = x2r.next()
            ucs = []
            for half in range(2):
                ci = j + half * NFF
                wb, t_wb = wr.next()
                fw.dma(fw.pool, wb, wsrc_cols(w_up_d, ci * 128, 128), writes=[t_wb])
                u, t_u = ur.next()
                for tt in range(NTT):
                    ps, t_ps = c.ps.next()
                    for kc in range(8):
                        fw.op(fw.pe, lambda e, kc=kc, wb=wb, ps=ps, tt=tt: e.matmul(ps[:, :], lhsT=wb[:, kc, :], rhs=h2T[:, kc, tt * 512:(tt + 1) * 512],
                                                                                   start=(kc == 0), stop=(kc == 7)), reads=[t_wb, t_h2], writes=[t_ps])
                    fw.op(fw.act, lambda e, ps=ps, u=u, tt=tt: e.activation(out=u[:, 1 + tt * 512:1 + (tt + 1) * 512], in_=ps[:, :], func=AF.Copy), reads=[t_ps], writes=[t_u])
                ps, t_ps = c.ps.next()
                for kc in range(8):
                    fw.op(fw.pe, lambda e, kc=kc, wb=wb, ps=ps: e.matmul(ps[:, 0:2], lhsT=wb[:, kc, :], rhs=h2T[:, kc, T:T + 2], start=(kc == 0), stop=(kc == 7)),
                          reads=[t_wb, t_h2], writes=[t_ps])
                fw.op(fw.act, lambda e, ps=ps, u=u: e.activation(out=u[:, 0:1], in_=ps[:, 0:1], func=AF.Copy), reads=[t_ps], writes=[t_u])
                fw.op(fw.act, lambda e, ps=ps, u=u: e.activation(out=u[:, T + 1:T + 2], in_=ps[:, 1:2], func=AF.Copy), reads=[t_ps], writes=[t_u])
                uc, t_uc = ucr.next()
                fw.op(fw.act, lambda e, u=u, uc=uc, ci=ci: e.activation(out=uc, in_=u[:, 1:T + 1], func=AF.Identity, scale=cw[:, 1, ci:ci + 1], bias=cb[:, ci:ci + 1]),
                      reads=[t_u, t_cw, t_cb], writes=[t_uc])
                fw.op(fw.dve, lambda e, u=u, uc=uc, ci=ci: e.scalar_tensor_tensor(out=uc, in0=u[:, 0:T], scalar=cw[:, 0, ci:ci + 1], in1=uc, op0=ALU.mult, op1=ALU.add),
                      reads=[t_u, t_cw, t_uc], writes=[t_uc])
                fw.op(fw.dve, lambda e, u=u, uc=uc, ci=ci: e.scalar_tensor_tensor(out=uc, in0=u[:, 2:T + 2], scalar=cw[:, 2, ci:ci + 1], in1=uc, op0=ALU.mult, op1=ALU.add),
                      reads=[t_u, t_cw, t_uc], writes=[t_uc])
                ucs.append((uc, t_uc))
            (ua, t_ua), (ug, t_ug) = ucs
            fw.op(fw.act, lambda e: e.activation(out=x2, in_=ua, func=AF.Gelu_apprx_tanh), reads=[t_ua], writes=[t_x2])
            fw.op(fw.dve, lambda e: e.tensor_tensor(out=actT[:, j, :], in0=x2, in1=ug, op=ALU.mult), reads=[t_x2, t_ug], writes=[t_act])
        A.release(m1)
        wd, t_wd = A.alloc([NFF, 1024], BF16)
        for j0 in range(0, NFF, 2):
            fw.dma(fw.pool, wd[:, j0:j0 + 2, :], w_down_d[j0 * 128:(j0 + 2) * 128, :].rearrange("(j p) n -> p j n", p=128), writes=[t_wd])
        if I.x_store is not None:
            gnext, t_gnext = load_vecT(c, I.norm_next, 8)
        gfin, t_gfin = A.alloc([1024], F32)
        fw.dma(fw.sp, gfin, nf_d.partition_broadcast(128), writes=[t_gfin])
        xr = Ring([A.alloc([1024], F32) for _ in range(3)])
        yr = Ring([A.alloc([1024], F32) for _ in range(2)])
        junk, t_junk = A.alloc([1024], BF16)
        ssr = Ring([A.alloc([1], F32) for _ in range(2)])
        pend = None
        for i in range(NT):
            xt, t_x = xr.next()
            I.x_load(i, xt, t_x)
            for half in range(2):
                ps, t_ps = c.ps.next()
                for j in range(NFF):
                    fw.op(fw.pe, lambda e, j=j, ps=ps, half=half, i=i: e.matmul(ps[:, :], lhsT=actT[:, j, i * 128:(i + 1) * 128], rhs=wd[:, j, half * 512:(half + 1) * 512],
                                                                               start=(j == 0), stop=(j == NFF - 1)), reads=[t_act, t_wd], writes=[t_ps])
                fw.op(fw.dve, lambda e, ps=ps, xt=xt, half=half: e.tensor_tensor(out=xt[:, half * 512:(half + 1) * 512], in0=xt[:, half * 512:(half + 1) * 512], in1=ps[:, :], op=ALU.add),
                      reads=[t_ps, t_x], writes=[t_x])
            if I.x_store is not None:
                I.x_store(i, xt, t_x)
                if pend is not None:
                    norm_tile_b(c, i - 1, *pend, gnext, t_gnext, I.HT, I.t_HT)
                pend = norm_tile_a(c, I.NS, xt, t_x)
            if I.y_store is None:
                continue
            ss, t_ss = ssr.next()
            fw.op(fw.act, lambda e, xt=xt, ss=ss: e.activation(out=junk, in_=xt, func=AF.Square, accum_out=ss), reads=[t_x], writes=[t_junk, t_ss])
            fw.op(fw.act, lambda e, ss=ss: e.activation(out=ss, in_=ss, func=AF.Sqrt, scale=1.0 / 1024, bias=EPS), reads=[t_ss], writes=[t_ss])
            fw.op(fw.dve, lambda e, ss=ss: e.reciprocal(out=ss, in_=ss), reads=[t_ss], writes=[t_ss])
            yt, t_y = yr.next()
            fw.op(fw.dve, lambda e, xt=xt, yt=yt, ss=ss: e.scalar_tensor_tensor(out=yt, in0=xt, scalar=ss[:, 0:1], in1=gfin, op0=ALU.mult, op1=ALU.mult),
                  reads=[t_x, t_ss, t_gfin], writes=[t_y])
            I.y_store(i, yt, t_y)
        if pend is not None:
            norm_tile_b(c, NT - 1, *pend, gnext, t_gnext, I.HT, I.t_HT)
        A.release(m0)


def na_bias_table(rpb, half):
    base = 32 * half
    qc = np.arange(64)
    col_start = np.clip(qc - 8, 0, 48)
    kp = np.arange(128)
    krow_off = kp // 64
    kcol = kp % 64
    colvalid = (kcol[:, None] >= col_start[None, :]) & (kcol[:, None] < col_start[None, :] + 16)
    dc = np.clip(kcol[:, None] - qc[None, :], -15, 15) + 15
    out = np.full((8, 128, NSLOT, 64), NEG, np.float32)
    sbase = na_slot_base()
    done = set()
    for j in range(32):
        b0 = sbase[j]
        if b0 in done:
            continue
        done.add(b0)
        r = base + j
        start = int(np.clip(r - 4, 0, 56))
        for s, p in enumerate(na_tiles(j)):
            R = base - 4 + 2 * p + krow_off
            rowvalid = (R >= start) & (R < start + 8)
            dr = np.clip(R - r + 7, 0, 14)
            valid = rowvalid[:, None] & colvalid
            vals = rpb[:, dr[:, None], dc]
            out[:, :, b0 + s, :] = np.where(valid[None], vals, np.float32(NEG))
    return np.ascontiguousarray(out.reshape(4, 2, 128, NSLOT, 64).transpose(0, 2, 3, 1, 4).reshape(4, 128, NSLOT, 128))


PAIRS = [[0, 1], [2, 3], [4, 5], [6, 7]]


def build_fused():
    nc = bass.Bass("TRN2", target_bir_lowering=False)
    dt = lambda name, shape, dtype=F32, kind="ExternalInput": nc.dram_tensor(name, shape, dtype, kind=kind).ap()
    x_d = dt("x", [T, D])
    xp_d = dt("xp", [T, D])
    mem_d = dt("mem", [256, D])
    norm_mix_d = dt("norm_mix", [DEPTH, D])
    w_in_d = dt("w_in", [DEPTH, D, IN_COLS])
    lbl_d = dt("lb_logits", [DEPTH, 2, D])
    lbsel_d = dt("lbsel", [128, DEPTH, DEPTH])
    gnorm_d = dt("gnorm", [DEPTH, 128])
    w_hg_o_d = dt("w_hg_o", [DEPTH, D, D])
    nab_d = dt("na_bias", [DEPTH, 4, 128, NSLOT, 128])
    w_na_o_d = dt("w_na_o", [DEPTH, 512, D])
    mem_norm_d = dt("mem_norm", [D])
    w_kv_d = dt("w_mem_kv", [DEPTH, D, D])
    w_ca_o_d = dt("w_ca_o", [DEPTH, 512, D])
    w_out_d = dt("w_out", [DEPTH, D, D])
    sel_d = dt("sel", [128, 2])
    selh_d = dt("selh", [2, 1])
    norm_ffn_d = dt("norm_ffn", [DEPTH, D])
    w_up_d = dt("w_up", [DEPTH, D, 2 * D_FF])
    conv_w_d = dt("conv_w", [DEPTH, 3, 2 * D_FF])
    conv_b_d = dt("conv_b", [DEPTH, 2 * D_FF])
    w_down_d = dt("w_down", [DEPTH, D_FF, D])
    nf_d = dt("norm_final", [D])
    ident_d = dt("ident", [128, 128])
    tri_d = dt("tri", [128, 2, 64])
    y_d = dt("y", [T, D], kind="ExternalOutput")
    it = lambda name, shape, dtype=F32: dt(name, shape, dtype, kind="Internal")
    og_d, ona_d, oca_d = it("og_s", [D, T], BF16), it("ona_s", [512, T], BF16), it("oca_s", [512, T], BF16)
    xmid_d = [it(f"xmid{l}", [T, D]) for l in range(DEPTH)]
    CH = 512
    NCH = T // CH
    xnext_d = [[it(f"xnext{l}_{k}", [CH, D]) for k in range(NCH)] for l in range(DEPTH - 1)]
    xg_d = [[it(f"xg{l}_{k}", [2 * CH, D]) for k in range(NCH)] for l in range(DEPTH - 1)]
    hbi_d = [it(f"hbi{l}", [2, D]) for l in range(DEPTH)]
    hbo_d = [it(f"hbo{l}", [4, D]) for l in range(DEPTH)]

    with ExitStack() as ctx:
        c = setup(nc, ctx, 53000)
        fw, A = c.fw, c.A
        block = ctx.enter_context(nc.Block())
        load_consts(c, ident_d, tri_d)
        sel, t_sel = A.alloc([2], F32)
        fw.dma(fw.sp, sel, sel_d[:, :], writes=[t_sel])
        selh, t_selh = A.alloc([1], F32)
        fw.dma(fw.sp, selh[:2], selh_d[:, :], writes=[t_selh])

        HT, t_HT = A.alloc([8, T + 2], BF16)
        NS = norm_scratch(c)

        def store_to(dst):
            def f(i, xt, t_x):
                fw.dma(fw.sp, dst[i * 128:(i + 1) * 128, :], xt, reads=[t_x])
            return f

        def store_chunks(dsts):
            def f(i, xt, t_x):
                r = (i * 128) % CH
                fw.dma(fw.sp, dsts[(i * 128) // CH][r:r + 128, :], xt, reads=[t_x])
            return f

        def load_chunks(srcs):
            def f(i, xt, t_x):
                r = (i * 128) % CH
                fw.dma(fw.sp, xt, srcs[(i * 128) // CH][r:r + 128, :], writes=[t_x])
            return f

        xg_deps = None
        for l in range(DEPTH):
            mk = A.mark()
            I = Ctx()
            if l == 0:
                I.x_load = dram_loader(c, x_d)
                I.xp_load = dram_loader(c, xp_d)
            else:
                I.x_load = load_chunks(xnext_d[l - 1])
                xb, t_xb = A.alloc([1024], F32)
                xg = xg_d[l - 1]

                def xp_load(i, xt, t_x, xg=xg, xb=xb, t_xb=t_xb):
                    r = (i * 128) % CH
                    g = xg[(i * 128) // CH]
                    fw.sp.wait(xg_deps[(i * 128) // CH])
                    fw.dma(fw.sp, xt, g[r:r + 128, :], writes=[t_x])
                    fw.dma(fw.sp, xb, g[CH + r:CH + r + 128, :], writes=[t_xb])
                    fw.op(fw.dve, lambda e: e.tensor_scalar(out=xt, in0=xt, scalar1=sel[:, 0:1], scalar2=None, op0=ALU.mult), reads=[t_x, t_sel], writes=[t_x])
                    fw.op(fw.dve, lambda e: e.scalar_tensor_tensor(out=xt, in0=xb, scalar=sel[:, 1:2], in1=xt, op0=ALU.mult, op1=ALU.add),
                          reads=[t_xb, t_sel, t_x], writes=[t_x])
                I.xp_load = xp_load
            I.HT, I.t_HT, I.NS, I.h_ready, I.norm_ffn = HT, t_HT, NS, (l > 0), norm_ffn_d[l]
            I.x_store = store_to(xmid_d[l])
            I.mem, I.norm_mix, I.w_in, I.lb_logits, I.lbsel = mem_d, norm_mix_d[l], w_in_d[l], lbl_d, lbsel_d[:, l, :]
            I.gnorm, I.w_hg_o, I.na_bias, I.w_na_o, I.mem_norm = gnorm_d[l], w_hg_o_d[l], nab_d[l], w_na_o_d[l], mem_norm_d
            I.w_mem_kv, I.w_ca_o, I.w_out, I.sel = w_kv_d[l], w_ca_o_d[l], w_out_d[l], sel_d[:, :]
            I.og_s, I.ona_s, I.oca_s = og_d, ona_d, oca_d
            emit_mixer(c, I)
            A.release(mk)
            fw.dma(fw.sp, hbi_d[l][0:1, :], xmid_d[l][0:1, :])
            fw.dma(fw.sp, hbi_d[l][1:2, :], xmid_d[l][T - 1:T, :])
            hb_dep = fw.allgather(hbi_d[l][:, :], hbo_d[l][:, :], PAIRS)
            J = Ctx()
            J.x_load = dram_loader(c, xmid_d[l])

            def halo_load(i, xt, t_x, hbo=hbo_d[l], hb_dep=hb_dep):
                fw.sp.wait(hb_dep)
                fw.dma(fw.sp, xt[:2], hbo[1:3, :], writes=[t_x])
                fw.op(fw.dve, lambda e: e.tensor_scalar(out=xt[:2], in0=xt[:2], scalar1=selh[:2, 0:1], scalar2=None, op0=ALU.mult), reads=[t_x, t_selh], writes=[t_x])
            J.halo_load = halo_load
            last = (l == DEPTH - 1)
            J.HT, J.t_HT, J.NS = HT, t_HT, NS
            J.norm_next = None if last else norm_mix_d[l + 1]
            J.x_store = None if last else store_chunks(xnext_d[l])
            J.y_store = store_to(y_d) if last else None
            J.norm_ffn, J.w_up, J.conv_w, J.conv_b, J.w_down, J.norm_final = norm_ffn_d[l], w_up_d[l], conv_w_d[l], conv_b_d[l], w_down_d[l], nf_d
            emit_ffn(c, J)
            if not last:
                xg_deps = [fw.allgather(xnext_d[l][k][:, :], xg_d[l][k][:, :], PAIRS) for k in range(NCH)]
        fw.finish()
        fw.emit(block)
    return nc


_PROG = []


def kernel(x, mem, norm_mix, w_in, hg_lb_logits, hg_gnorm, w_hg_o, na_rpb, w_na_o, mem_norm,
           w_mem_kv, w_ca_o, w_out, norm_ffn, w_up, conv_w, conv_b, w_down, norm_final):
    f = lambda a: np.ascontiguousarray(np.asarray(a, dtype=np.float32))
    x = f(x)
    B = x.shape[0]
    ident = np.eye(128, dtype=np.float32)
    s_idx = np.arange(128) % 64
    t_idx = np.arange(64)
    tri = np.stack([(s_idx[:, None] <= t_idx[None, :]), (s_idx[:, None] >= t_idx[None, :])], axis=1).astype(np.float32)
    cores = [(b, hf) for b in range(B) for hf in range(2)]
    lbsel = np.zeros((128, DEPTH, DEPTH), np.float32)
    for l in range(DEPTH):
        lbsel[:, l, 1:l + 1] = 1.0
    shared = {
        "norm_mix": f(norm_mix), "w_in": f(w_in), "lb_logits": f(hg_lb_logits), "lbsel": lbsel, "gnorm": f(hg_gnorm),
        "w_hg_o": f(w_hg_o), "w_na_o": f(w_na_o), "mem_norm": f(mem_norm), "w_mem_kv": f(w_mem_kv), "w_ca_o": f(w_ca_o),
        "w_out": f(w_out), "norm_ffn": f(norm_ffn), "w_up": f(w_up), "conv_w": f(conv_w), "conv_b": f(conv_b),
        "w_down": f(w_down), "norm_final": f(norm_final), "ident": ident, "tri": tri,
    }
    nab = [np.stack([na_bias_table(f(na_rpb[l]), hf) for l in range(DEPTH)]) for hf in range(2)]
    in_maps = []
    for (b, hf) in cores:
        sel = np.zeros((128, 2), np.float32)
        sel[:, 0] = 1.0 if hf == 1 else 0.0
        sel[:, 1] = 1.0 if hf == 0 else 0.0
        selh = np.array([[1.0 if hf == 1 else 0.0], [1.0 if hf == 0 else 0.0]], np.float32)
        m = dict(shared)
        m.update({"x": f(x[b, hf * T:(hf + 1) * T]), "xp": f(x[b, (1 - hf) * T:(2 - hf) * T]), "mem": f(mem[b]),
                  "na_bias": nab[hf], "sel": sel, "selh": selh})
        in_maps.append(m)
    if not _PROG:
        _PROG.append(build_fused())
    res = run_bass_kernel_spmd(_PROG[0], in_maps, core_ids=list(range(8)))
    y = np.empty_like(x)
    for ci, (b, hf) in enumerate(cores):
        y[b, hf * T:(hf + 1) * T] = res.results[ci]["y"]
    return y
```
